# Optimizing a Trainium2 kernel written in Bass

```python
import math
import jax, jax.numpy as jnp
from jax import lax
import numpy as np

D_MODEL = 2048
BATCH = 4
SEQ = 8192
DEPTH = 1
DEC_BATCH = 4
DEC_SEQ = 2048
PAST_LEN = 128

PLE_DIM = 256
MLA_HEADS = 8
Q_LORA = 512
KV_LORA = 256
QK_NOPE = 128
QK_ROPE = 64
V_HEAD = 128
ROPE_BASE = 10000.0
Q_BLOCK = 128
GDN_HEADS = 8
GDN_DK = 128
GDN_DV = 128
CONV_K = 5
CHUNK = 64
D_FF = 5632
EPS = 1e-6

MLA_WIDTH = MLA_HEADS * V_HEAD
GDN_WIDTH = GDN_HEADS * GDN_DV
MIX_WIDTH = MLA_WIDTH + GDN_WIDTH
GDN_QK = GDN_HEADS * GDN_DK
GDN_V = GDN_HEADS * GDN_DV
IN_SIZES = (Q_LORA, KV_LORA, QK_ROPE, GDN_QK, GDN_QK, GDN_V, GDN_V,
            GDN_HEADS, GDN_HEADS, GDN_HEADS, GDN_HEADS)
IN_DIM = int(sum(IN_SIZES))
SPLIT_POINTS = tuple(int(s) for s in np.cumsum(IN_SIZES)[:-1])

kernel_name = "hybrid_mla_gdn_macaron_encoder"


def rmsnorm(x, g):
    xf = x.astype(jnp.float32)
    y = xf * lax.rsqrt(jnp.mean(xf * xf, axis=-1, keepdims=True) + EPS)
    return (y * g.astype(jnp.float32)).astype(x.dtype)


def swiglu(x, w_gate, w_up, w_down):
    return (jax.nn.silu(x @ w_gate) * (x @ w_up)) @ w_down


def rope_tables(seq):
    inv = 1.0 / (ROPE_BASE ** (jnp.arange(0, QK_ROPE, 2, dtype=jnp.float32) / QK_ROPE))
    ang = jnp.arange(seq, dtype=jnp.float32)[:, None] * inv[None, :]
    return jnp.cos(ang), jnp.sin(ang)


def apply_rope(x, cos, sin):
    extra = x.ndim - 3
    shp = (cos.shape[0],) + (1,) * extra + (cos.shape[1],)
    c, s = cos.reshape(shp), sin.reshape(shp)
    xf = x.astype(jnp.float32)
    x1, x2 = xf[..., : QK_ROPE // 2], xf[..., QK_ROPE // 2:]
    return jnp.concatenate([x1 * c - x2 * s, x2 * c + x1 * s], axis=-1).astype(x.dtype)


def mla_mix(c_q, c_kv, k_rope, q_norm, w_uq, kv_norm, w_ukv):
    B, S, _ = c_q.shape
    q = (rmsnorm(c_q, q_norm) @ w_uq).reshape(B, S, MLA_HEADS, QK_NOPE + QK_ROPE)
    kv = (rmsnorm(c_kv, kv_norm) @ w_ukv).reshape(B, S, MLA_HEADS, QK_NOPE + V_HEAD)
    q_nope, q_rope = q[..., :QK_NOPE], q[..., QK_NOPE:]
    k_nope, v = kv[..., :QK_NOPE], kv[..., QK_NOPE:]
    cos, sin = rope_tables(S)
    q_rope = apply_rope(q_rope, cos, sin)
    k_rope = apply_rope(k_rope, cos, sin)
    scale = (QK_NOPE + QK_ROPE) ** -0.5
    nblk = S // Q_BLOCK
    qn_b = q_nope.reshape(B, nblk, Q_BLOCK, MLA_HEADS, QK_NOPE).transpose(1, 0, 2, 3, 4)
    qr_b = q_rope.reshape(B, nblk, Q_BLOCK, MLA_HEADS, QK_ROPE).transpose(1, 0, 2, 3, 4)

    def block(args):
        qn, qr = args
        s = (jnp.einsum('bqhd,bkhd->bhqk', qn, k_nope)
             + jnp.einsum('bqhr,bkr->bhqk', qr, k_rope)).astype(jnp.float32)
        p = jax.nn.softmax(s * scale, axis=-1)
        return jnp.einsum('bhqk,bkhd->bqhd', p.astype(v.dtype), v)

    o = lax.map(block, (qn_b, qr_b))
    return o.transpose(1, 0, 2, 3, 4).reshape(B, S, MLA_WIDTH)


def centered_conv(x, w):
    C = x.shape[-1]
    return lax.conv_general_dilated(
        x, w[:, None, :].astype(x.dtype), window_strides=(1,),
        padding=[(CONV_K // 2, CONV_K // 2)],
        dimension_numbers=('NWC', 'WIO', 'NWC'), feature_group_count=C)


def l2norm(x):
    return x * lax.rsqrt(jnp.sum(x * x, axis=-1, keepdims=True) + EPS)


def gated_delta_chunked(q, k, v, beta, g):
    B, S, H, DK = q.shape
    DV = v.shape[-1]
    N, C = S // CHUNK, CHUNK

    def chunks(t):
        return t.reshape((B, N, C, H) + t.shape[3:]).swapaxes(2, 3)

    q, k, v, beta, g = chunks(q), chunks(k), chunks(v), chunks(beta), chunks(g)
    G = jnp.cumsum(g, axis=-1)
    tri_incl = jnp.tril(jnp.ones((C, C), dtype=bool))
    tri_strict = jnp.tril(jnp.ones((C, C), dtype=bool), -1)
    decay = jnp.exp(jnp.where(tri_incl, G[..., :, None] - G[..., None, :], -jnp.inf))
    kk = jnp.einsum('bnhcd,bnhed->bnhce', k, k)
    A = jnp.where(tri_strict, beta[..., :, None] * kk * decay, 0.0)
    rhs = jnp.concatenate([v * beta[..., None], k * (beta * jnp.exp(G))[..., None]], axis=-1)
    sol = lax.linalg.triangular_solve(A, rhs, left_side=True, lower=True, unit_diagonal=True)
    u, w = sol[..., :DV], sol[..., DV:]
    qk = jnp.einsum('bnhcd,bnhed->bnhce', q, k) * decay
    q_dec = q * jnp.exp(G)[..., None]
    k_dec = k * jnp.exp(G[..., -1:] - G)[..., None]
    last = jnp.exp(G[..., -1])

    def step(state, xs):
        u_c, w_c, qk_c, qd_c, kd_c, last_c = xs
        v_new = u_c - jnp.einsum('bhcd,bhde->bhce', w_c, state)
        o = (jnp.einsum('bhcd,bhde->bhce', qd_c, state)
             + jnp.einsum('bhce,bhef->bhcf', qk_c, v_new))
        state = state * last_c[..., None, None] + jnp.einsum('bhcd,bhce->bhde', kd_c, v_new)
        return state, o

    xs = tuple(t.swapaxes(0, 1) for t in (u, w, qk, q_dec, k_dec, last))
    s0 = jnp.zeros((B, H, DK, DV), jnp.float32)
    _, o = lax.scan(step, s0, xs)
    return o.transpose(1, 0, 3, 2, 4).reshape(B, S, H, DV)


def gdn_mix(gq, gk, gv, gz, a_f, a_b, b_f, b_b, conv_w, a_log, dt_bias, gdn_norm):
    B, S, _ = gq.shape
    dtype = gq.dtype
    qkv = jax.nn.silu(centered_conv(jnp.concatenate([gq, gk, gv], axis=-1), conv_w))
    q = qkv[..., :GDN_QK].reshape(B, S, GDN_HEADS, GDN_DK).astype(jnp.float32)
    k = qkv[..., GDN_QK:2 * GDN_QK].reshape(B, S, GDN_HEADS, GDN_DK).astype(jnp.float32)
    v = qkv[..., 2 * GDN_QK:].reshape(B, S, GDN_HEADS, GDN_DV).astype(jnp.float32)
    q = l2norm(q) * (GDN_DK ** -0.5)
    k = l2norm(k)
    a_log = a_log.astype(jnp.float32)
    dt_bias = dt_bias.astype(jnp.float32)
    g_f = -jnp.exp(a_log[0]) * jax.nn.softplus(a_f.astype(jnp.float32) + dt_bias[0])
    beta_f = jax.nn.sigmoid(b_f.astype(jnp.float32))
    o = gated_delta_chunked(q, k, v, beta_f, g_f)
    g_b = -jnp.exp(a_log[1]) * jax.nn.softplus(a_b.astype(jnp.float32) + dt_bias[1])
    beta_b = jax.nn.sigmoid(b_b.astype(jnp.float32))
    fl = lambda t: jnp.flip(t, axis=1)
    o = o + fl(gated_delta_chunked(fl(q), fl(k), fl(v), fl(beta_b), fl(g_b)))
    z = gz.reshape(B, S, GDN_HEADS, GDN_DV).astype(jnp.float32)
    o = o * lax.rsqrt(jnp.mean(o * o, axis=-1, keepdims=True) + EPS)
    o = o * gdn_norm.astype(jnp.float32) * jax.nn.silu(z)
    return o.reshape(B, S, GDN_WIDTH).astype(dtype)


def encoder_layer(h, p_l, norm_ffn1, ffn1_w_gate, ffn1_w_up, ffn1_w_down, norm_mix, w_in,
                  q_norm, w_uq, kv_norm, w_ukv, conv_w, a_log, dt_bias, gdn_norm, w_out,
                  norm_ffn2, ffn2_w_gate, ffn2_w_up, ffn2_w_down, norm_ple, w_ple, w_ple_gate):
    h = h + 0.5 * swiglu(rmsnorm(h, norm_ffn1), ffn1_w_gate, ffn1_w_up, ffn1_w_down)
    n = rmsnorm(h, norm_mix)
    parts = jnp.split(n @ w_in, SPLIT_POINTS, axis=-1)
    c_q, c_kv, k_rope, gq, gk, gv, gz, a_f, a_b, b_f, b_b = parts
    y_mla = mla_mix(c_q, c_kv, k_rope, q_norm, w_uq, kv_norm, w_ukv)
    y_gdn = gdn_mix(gq, gk, gv, gz, a_f, a_b, b_f, b_b, conv_w, a_log, dt_bias, gdn_norm)
    h = h + jnp.concatenate([y_mla, y_gdn], axis=-1) @ w_out
    h = h + 0.5 * swiglu(rmsnorm(h, norm_ffn2), ffn2_w_gate, ffn2_w_up, ffn2_w_down)
    gate = jax.nn.sigmoid((rmsnorm(h, norm_ple) @ w_ple_gate).astype(jnp.float32)).astype(h.dtype)
    h = h + (p_l @ w_ple) * gate
    return h


def trunk(x, p, layer_w, norm_final):
    h = x
    for i in range(DEPTH):
        h = encoder_layer(h, p[i], *[w[i] for w in layer_w])
    return rmsnorm(h, norm_final)


def setup_inputs(seed: int = 0) -> dict:
    key = jax.random.key(seed)
    ks = iter(jax.random.split(key, 40))
    f32 = jnp.float32

    def nrm(shape, scale):
        return jax.random.normal(next(ks), shape, f32) * scale

    def gain(shape):
        return 1.0 + 0.02 * jax.random.normal(next(ks), shape, f32)

    L = DEPTH
    x_prompt = jax.random.normal(next(ks), (BATCH, SEQ, D_MODEL), f32)
    x_sample = jax.random.normal(next(ks), (DEC_BATCH, DEC_SEQ, D_MODEL), f32)
    p_prompt = jax.random.normal(next(ks), (DEPTH, BATCH, SEQ, PLE_DIM), f32)
    p_sample = jax.random.normal(next(ks), (DEPTH, DEC_BATCH, DEC_SEQ, PLE_DIM), f32)
    a_log = jnp.log(jax.random.uniform(next(ks), (L, 2, GDN_HEADS), f32, 1.0, 16.0))
    dt = jnp.exp(jax.random.uniform(next(ks), (L, 2, GDN_HEADS), f32,
                                    math.log(1e-3), math.log(1e-1)))
    dt_bias = dt + jnp.log(-jnp.expm1(-dt))
    return {
        "x_prompt": x_prompt,
        "x_sample": x_sample,
        "p_prompt": p_prompt,
        "p_sample": p_sample,
        "norm_ffn1": gain((L, D_MODEL)),
        "ffn1_w_gate": nrm((L, D_MODEL, D_FF), D_MODEL ** -0.5),
        "ffn1_w_up": nrm((L, D_MODEL, D_FF), D_MODEL ** -0.5),
        "ffn1_w_down": nrm((L, D_FF, D_MODEL), D_FF ** -0.5),
        "norm_mix": gain((L, D_MODEL)),
        "w_in": nrm((L, D_MODEL, IN_DIM), D_MODEL ** -0.5),
        "q_norm": gain((L, Q_LORA)),
        "w_uq": nrm((L, Q_LORA, MLA_HEADS * (QK_NOPE + QK_ROPE)), Q_LORA ** -0.5),
        "kv_norm": gain((L, KV_LORA)),
        "w_ukv": nrm((L, KV_LORA, MLA_HEADS * (QK_NOPE + V_HEAD)), KV_LORA ** -0.5),
        "conv_w": nrm((L, CONV_K, 2 * GDN_QK + GDN_V), CONV_K ** -0.5),
        "a_log": a_log,
        "dt_bias": dt_bias,
        "gdn_norm": gain((L, GDN_DV)),
        "w_out": nrm((L, MIX_WIDTH, D_MODEL), MIX_WIDTH ** -0.5),
        "norm_ffn2": gain((L, D_MODEL)),
        "ffn2_w_gate": nrm((L, D_MODEL, D_FF), D_MODEL ** -0.5),
        "ffn2_w_up": nrm((L, D_MODEL, D_FF), D_MODEL ** -0.5),
        "ffn2_w_down": nrm((L, D_FF, D_MODEL), D_FF ** -0.5),
        "norm_ple": gain((L, D_MODEL)),
        "w_ple": nrm((L, PLE_DIM, D_MODEL), PLE_DIM ** -0.5),
        "w_ple_gate": nrm((L, D_MODEL, D_MODEL), D_MODEL ** -0.5),
        "norm_final": gain((D_MODEL,)),
    }


def reference(x_prompt, x_sample, p_prompt, p_sample, norm_ffn1, ffn1_w_gate, ffn1_w_up,
              ffn1_w_down, norm_mix, w_in, q_norm, w_uq, kv_norm, w_ukv, conv_w, a_log,
              dt_bias, gdn_norm, w_out, norm_ffn2, ffn2_w_gate, ffn2_w_up, ffn2_w_down,
              norm_ple, w_ple, w_ple_gate, norm_final):
    layer_w = (norm_ffn1, ffn1_w_gate, ffn1_w_up, ffn1_w_down, norm_mix, w_in,
               q_norm, w_uq, kv_norm, w_ukv, conv_w, a_log, dt_bias, gdn_norm, w_out,
               norm_ffn2, ffn2_w_gate, ffn2_w_up, ffn2_w_down, norm_ple, w_ple, w_ple_gate)
    y_prompt = trunk(x_prompt, p_prompt, layer_w, norm_final)
    y_sample = trunk(x_sample, p_sample, layer_w, norm_final)
    return (y_prompt, y_sample)
```

```python
import numpy as np
from contextlib import ExitStack
import concourse.bass as bass
import concourse.mybir as mybir
from concourse.bass_utils import run_bass_kernel_spmd

F32 = mybir.dt.float32
BF16 = mybir.dt.bfloat16
F32R = mybir.dt.float32r
AF = mybir.ActivationFunctionType
ALU = mybir.AluOpType

D = 2048
FF = 5632
KC = D // 128
NFC = FF // 128
PLE = 256
H = 8
EPS = 1e-6
NDS = 24
IN_DIM = 4960


class Buf:
    __slots__ = ("w", "r")

    def __init__(self):
        self.w = {}
        self.r = {}


class KB:
    def __init__(self, nc, es):
        self.nc = nc
        self.E = {}
        for name, eng in (("pe", nc.tensor), ("act", nc.scalar), ("dve", nc.vector),
                          ("pool", nc.gpsimd), ("sp", nc.sync)):
            self.E[name] = dict(eng=eng, sem=es.enter_context(nc.semaphore("s_" + name)), cnt=0, seen={})
        self.dsems = [es.enter_context(nc.semaphore("d%d" % i)) for i in range(NDS)]
        self.dtot = [0] * NDS
        self.dnext = 0
        self.dnext_p = 0
        self.rr = 0

    def _sem(self, key):
        return self.E[key]["sem"] if isinstance(key, str) else self.dsems[key[1]]

    def wait(self, en, key, val):
        E = self.E[en]
        if E["seen"].get(key, 0) >= val:
            return
        E["eng"].wait_ge(self._sem(key), val)
        E["seen"][key] = val

    def _deps(self, en, r, w):
        toks = {}
        for b in r:
            for k, v in b.w.items():
                if toks.get(k, 0) < v:
                    toks[k] = v
        for b in w:
            for k, v in b.w.items():
                if toks.get(k, 0) < v:
                    toks[k] = v
            for k, v in b.r.items():
                if toks.get(k, 0) < v:
                    toks[k] = v
        for k, v in toks.items():
            if k == en and en == "pe":
                continue
            self.wait(en, k, v)

    def _mark(self, tok, r, w):
        k, v = tok
        for b in r:
            if b.r.get(k, 0) < v:
                b.r[k] = v
        for b in w:
            if b.r:
                b.w = {k: v}
                b.r = {}
            else:
                b.w[k] = v

    def op(self, en, fn, r=(), w=(), inc=True):
        E = self.E[en]
        self._deps(en, r, w)
        inst = fn(E["eng"])
        tok = (en, E["cnt"] + 1)
        if inc:
            E["cnt"] += 1
            inst.then_inc(E["sem"], 1)
        self._mark(tok, r, w)
        return inst

    def dma(self, qn, out, in_, r=(), w=(), **kw):
        E = self.E[qn]
        self._deps(qn, r, w)
        if qn == "pool":
            i = NDS - 8 + self.dnext_p
            self.dnext_p = (self.dnext_p + 1) % 8
        else:
            i = self.dnext
            self.dnext = (i + 1) % (NDS - 8)
        if self.dtot[i] > 0:
            self.wait(qn, ("d", i), self.dtot[i])
        inst = E["eng"].dma_start(out=out, in_=in_, **kw)
        self.dtot[i] += 16
        inst.then_inc(self.dsems[i], 16)
        self._mark((("d", i), self.dtot[i]), r, w)

    def barrier(self):
        for en in self.E:
            for k2, E2 in self.E.items():
                if k2 != en and E2["cnt"] > 0:
                    self.wait(en, k2, E2["cnt"])
            for i in range(NDS):
                if self.dtot[i] > 0:
                    self.wait(en, ("d", i), self.dtot[i])

    def any3(self):
        self.rr = (self.rr + 1) % 3
        return ("dve", "pool", "act")[self.rr]

    def any2(self):
        self.rr = (self.rr + 1) % 2
        return ("dve", "act")[self.rr]


def carve(t, off, shape, dt):
    flat = t[:].rearrange("p a b -> p (a b)")
    nel = 1
    for v in shape[1:]:
        nel *= v
    esz = 4 if dt == F32 else 2
    v = flat[0:shape[0], off // 2: off // 2 + nel * esz // 2]
    if dt == F32:
        v = v.bitcast(F32)
    if len(shape) == 3:
        v = v.rearrange("p (a b) -> p a b", b=shape[2])
    return v


def merge(dsts, srcs):
    for dd in dsts:
        for ss in srcs:
            for src in (ss.w, ss.r):
                for k, v in src.items():
                    if dd.w.get(k, 0) < v:
                        dd.w[k] = v


def scale_cast(kb, en, out, in_, scal):
    if en == "act":
        kb_fn = lambda e: e.activation(out=out, in_=in_, func=AF.Copy, scale=scal)
    else:
        kb_fn = lambda e: e.tensor_scalar(out=out, in0=in_, scalar1=scal, scalar2=None, op0=ALU.mult)
    return kb_fn


def build(T, TT):
    NT = T // TT
    NS = TT // 128
    NCH = T // 128
    nc = bass.Bass("TRN2", target_bir_lowering=False)
    es = ExitStack()
    kb = KB(nc, es)

    def din(name, shape, dt=F32):
        return nc.dram_tensor(name, list(shape), dt, kind="ExternalInput").ap()

    def dscr(name, shape, dt=BF16):
        return nc.dram_tensor(name, list(shape), dt, kind="Internal").ap()

    x_in = din("x", [T, D])
    p_in = din("p", [T, PLE])
    kmask_in = din("kmask", [128, NCH])
    tmask_in = din("tmask", [T])
    cos_in = din("cos2", [64, T])
    sin_in = din("sin2", [64, T])
    W = {}
    for nm, shp in (("norm_ffn1", [D]), ("ffn1_w_gate", [D, FF]), ("ffn1_w_up", [D, FF]), ("ffn1_w_down", [FF, D]),
                    ("norm_mix", [D]), ("w_in", [D, IN_DIM]), ("q_norm", [512]), ("w_uq", [512, 1536]),
                    ("kv_norm", [256]), ("w_ukv", [256, 2048]), ("conv_w", [5, 3072]), ("a_log", [16]),
                    ("dt_bias", [16]), ("gdn_norm", [128]), ("w_out", [D, D]), ("norm_ffn2", [D]),
                    ("ffn2_w_gate", [D, FF]), ("ffn2_w_up", [D, FF]), ("ffn2_w_down", [FF, D]),
                    ("norm_ple", [D]), ("w_ple", [PLE, D]), ("w_ple_gate", [D, D]), ("norm_final", [D])):
        W[nm] = din(nm, shp)
    y_out = nc.dram_tensor("y", [T, D], F32, kind="ExternalOutput").ap()

    WGU = {k: dscr("s_" + k, [22, 128, KC, 256]) for k in ("g1", "u1", "g2", "u2")}
    WDN = {k: dscr("s_" + k, [4, 2, 128, NFC // 2, 512]) for k in ("d1", "d2")}
    WCQ = dscr("s_wcq", [128, KC, 768])
    WKR = dscr("s_wkr", [128, KC, 128])
    WGQ = dscr("s_wgq", [12, 128, KC, 256])
    WZ = dscr("s_wz", [2, 128, KC, 512])
    WAB = dscr("s_wab", [128, KC, 32])
    WUQ = dscr("s_wuq", [128, 4, 2048])
    WUKV = dscr("s_wukv", [128, 2, 2048])
    WOUT = dscr("s_wout", [4, 128, KC, 512])
    WPG = dscr("s_wpg", [4, 128, KC, 512])
    WPLE = dscr("s_wple", [128, 2, 2048])
    H1 = dscr("s_h1", [T, D], F32)
    QNT = dscr("s_qnt", [H, 128, T])
    QRT = dscr("s_qrt", [H, 64, T])
    KNT = dscr("s_knt", [H, 128, T])
    KRT = dscr("s_krt", [64, T])
    VTM = dscr("s_vtm", [T, H * 128])
    GQKV = dscr("s_gqkv", [24, 128, T], F32)
    ZTM = dscr("s_ztm", [T, 1024], F32)
    ABT = dscr("s_abt", [T, 32], F32)
    YT = dscr("s_yt", [KC, 128, T])
    GQT = dscr("s_gqt", [H, 128, T], F32)
    GKT = dscr("s_gkt", [H, 128, T], F32)
    GKTM = dscr("s_gktm", [T, H * 128], F32)
    GVTM = dscr("s_gvtm", [T, H * 128], F32)
    OF = dscr("s_of", [T, H * 128], F32)
    OB = dscr("s_ob", [T, H * 128], F32)

    uniq = {"n": 0}

    def sb(name, shape, dt=F32, stack=es):
        uniq["n"] += 1
        return stack.enter_context(nc.sbuf_tensor("%s_%d" % (name, uniq["n"]), list(shape), dt))

    idf = sb("idf", [128, 128])
    idb = sb("idb", [128, 128], BF16)
    onesb = sb("onesb", [128, 128], BF16)
    onesf = sb("onesf", [128, 128])
    m_le = sb("m_le", [128, 128])
    m_lt = sb("m_lt", [128, 128])
    m_ge = sb("m_ge", [128, 128])
    m_gt = sb("m_gt", [128, 128])
    cb = Buf()
    psf = [es.enter_context(nc.psum_tensor("psf%d" % i, [128, 512], F32)) for i in range(6)]
    psb = [es.enter_context(nc.psum_tensor("psb%d" % i, [128, 1024], BF16)) for i in range(2)]
    psf_b = [Buf() for _ in range(6)]
    psb_b = [Buf(), Buf()]
    state = {"bank": 0, "tb": 0}

    def bank():
        i = state["bank"]
        state["bank"] = (i + 1) % 6
        assert (not psf_b[i].w) or psf_b[i].r, "PSUM bank handed out again before its consumer was emitted"
        return psf[i], psf_b[i]

    def tbank():
        i = state["tb"]
        state["tb"] = 1 - i
        return psb[i][:, 0:512], psb_b[i]

    kb.op("pool", lambda e: e.memset(onesf[:], 1.0), w=[cb])
    for t, pat, cm, cmp_ in ((idf, [[-1, 128]], 1, ALU.is_equal), (m_le, [[1, 128]], -1, ALU.is_ge),
                             (m_lt, [[1, 128]], -1, ALU.is_gt), (m_ge, [[-1, 128]], 1, ALU.is_ge),
                             (m_gt, [[-1, 128]], 1, ALU.is_gt)):
        kb.op("pool", lambda e, t=t, pat=pat, cm=cm, cmp_=cmp_: e.affine_select(
            out=t[:], in_=onesf[:], pattern=pat, compare_op=cmp_, fill=0.0, base=0, channel_multiplier=cm),
            r=[cb], w=[cb])
    kb.op("dve", lambda e: e.tensor_copy(out=idb[:], in_=idf[:]), r=[cb], w=[cb])
    kb.op("dve", lambda e: e.tensor_copy(out=onesb[:], in_=onesf[:]), r=[cb], w=[cb])

    def mm(out, lhsT, rhs, rbufs, wbuf, start=True, stop=True):
        kb.op("pe", lambda e: e.matmul(out=out, lhsT=lhsT, rhs=rhs, start=start, stop=stop),
              r=rbufs, w=[wbuf], inc=stop)

    def rstd_of(ss, n, stack_tmp, name):
        pass

    with ExitStack() as ph:
        gains = {}
        gb = Buf()
        for nm, k in (("norm_ffn1", KC), ("norm_mix", KC), ("norm_ffn2", KC), ("norm_ple", KC),
                      ("q_norm", 4), ("kv_norm", 2)):
            g = sb("g_" + nm, [128, k], stack=ph)
            kb.dma("sp", g[:], W[nm].rearrange("(k p) -> p k", p=128), w=[gb], allow_slow_non_contiguous=True)
            gains[nm] = g
        NSTG = 2
        stg_f = [sb("stgf%d" % i, [128, FF], stack=ph) for i in range(NSTG)]
        stg_b = [sb("stgb%d" % i, [128, FF], BF16, stack=ph) for i in range(NSTG)]
        stg_fb = [Buf() for _ in range(NSTG)]
        stg_bb = [Buf() for _ in range(NSTG)]
        cnt = {"i": 0}

        def prep(src, nrows, ncols, gain, stores, neg_swap=None):
            for kc in range(nrows // 128):
                i = cnt["i"] % NSTG
                cnt["i"] += 1
                kb.dma("sp", stg_f[i][:, 0:ncols], src[kc * 128:(kc + 1) * 128, :], w=[stg_fb[i]])
                scal = gains[gain][:, kc:kc + 1] if gain else 1.0
                en = kb.any2()
                kb.op(en, scale_cast(kb, en, stg_b[i][:, 0:ncols], stg_f[i][:, 0:ncols], scal),
                      r=[stg_fb[i], gb], w=[stg_bb[i]])
                for dst, view in stores(kc, stg_b[i]):
                    kb.dma("pool", dst, view, r=[stg_bb[i]])

        for k, nm, gn in (("g1", "ffn1_w_gate", "norm_ffn1"), ("u1", "ffn1_w_up", "norm_ffn1"),
                          ("g2", "ffn2_w_gate", "norm_ffn2"), ("u2", "ffn2_w_up", "norm_ffn2")):
            dst = WGU[k].rearrange("g p k c -> p k g c")
            prep(W[nm], D, FF, gn,
                 lambda kc, s, dst=dst: [(dst[:, kc], s[:, 0:FF].rearrange("p (g c) -> p g c", c=256))])
        for k, nm in (("d1", "ffn1_w_down"), ("d2", "ffn2_w_down")):
            prep(W[nm], FF, D, None,
                 lambda kc, s, k=k: [(WDN[k][:, kc // 22, :, kc % 22, :].rearrange("g p c -> p g c"),
                                      s[:, 0:D].rearrange("p (g c) -> p g c", c=512))])
        wgq_v = WGQ.rearrange("g p k c -> p k g c")
        wz_v = WZ.rearrange("g p k c -> p k g c")

        def win_stores(kc, s):
            return [(WCQ[:, kc, :], s[:, 0:768]),
                    (WKR[:, kc, 0:64], s[:, 768:832]),
                    (WKR[:, kc, 64:96], s[:, 800:832]),
                    (WKR[:, kc, 96:128], s[:, 768:800]),
                    (wgq_v[:, kc], s[:, 832:3904].rearrange("p (g c) -> p g c", c=256)),
                    (wz_v[:, kc], s[:, 3904:4928].rearrange("p (g c) -> p g c", c=512)),
                    (WAB[:, kc, :], s[:, 4928:4960])]
        prep(W["w_in"], D, IN_DIM, "norm_mix", win_stores)

        def wuq_stores(kc, s):
            sv = s[:, 0:1536].rearrange("p (h c) -> p h c", c=192)
            return [(WUQ[:, kc, 0:1024].rearrange("p (h c) -> p h c", c=128), sv[:, :, 0:128]),
                    (WUQ[:, kc, 1024:1536].rearrange("p (h c) -> p h c", c=64), sv[:, :, 128:192]),
                    (WUQ[:, kc, 1536:2048].rearrange("p (h c) -> p h c", c=64)[:, :, 0:32], sv[:, :, 160:192]),
                    (WUQ[:, kc, 1536:2048].rearrange("p (h c) -> p h c", c=64)[:, :, 32:64], sv[:, :, 128:160])]
        prep(W["w_uq"], 512, 1536, "q_norm", wuq_stores)

        def wukv_stores(kc, s):
            sv = s[:, 0:2048].rearrange("p (h c) -> p h c", c=256)
            return [(WUKV[:, kc, 0:1024].rearrange("p (h c) -> p h c", c=128), sv[:, :, 0:128]),
                    (WUKV[:, kc, 1024:2048].rearrange("p (h c) -> p h c", c=128), sv[:, :, 128:256])]
        prep(W["w_ukv"], 256, 2048, "kv_norm", wukv_stores)
        for dstT, nm, gn in ((WOUT, "w_out", None), (WPG, "w_ple_gate", "norm_ple")):
            dst = dstT.rearrange("g p k c -> p k g c")
            prep(W[nm], D, D, gn,
                 lambda kc, s, dst=dst: [(dst[:, kc], s[:, 0:D].rearrange("p (g c) -> p g c", c=512))])
        prep(W["w_ple"], PLE, D, None, lambda kc, s: [(WPLE[:, kc, :], s[:, 0:D])])
        kb.barrier()

    def norm_T(ph_t, xt, xt_b, nT, nT_b, xn, xn_b, ssq, ssq_b, junk, junk_b, width, nk, s_list=None):
        for s in range(NS):
            kb.op("act", lambda e, s=s: e.activation(out=xn[:, 0, 0:width], in_=xt[:, s, 0:width], func=AF.Square,
                                                     accum_out=ssq[:, s:s + 1]),
                  r=[xt_b[s]], w=[xn_b[0], ssq_b[s]])
            kb.op("act", lambda e, s=s: e.activation(out=ssq[:, s:s + 1], in_=ssq[:, s:s + 1], func=AF.Sqrt,
                                                     scale=1.0 / width, bias=eps_t[:, 0:1]),
                  r=[cb], w=[ssq_b[s]])
            kb.op("dve", lambda e, s=s: e.reciprocal(out=ssq[:, s:s + 1], in_=ssq[:, s:s + 1]), w=[ssq_b[s]])
            en = "dve" if s % 2 else "act"
            kb.op(en, scale_cast(kb, en, xn[:, 0, 0:width], xt[:, s, 0:width], ssq[:, s:s + 1]),
                  r=[xt_b[s], ssq_b[s]], w=[xn_b[0]])
            for k0 in range(0, nk, 4):
                pt, pt_b = tbank()
                for kk in range(4):
                    kc = k0 + kk
                    kb.op("pe", lambda e, kk=kk, kc=kc, pt=pt: e.transpose(
                        out=pt[:, kk * 128:(kk + 1) * 128], in_=xn[:, 0, kc * 128:(kc + 1) * 128], identity=idb[:]),
                        r=[xn_b[0], cb], w=[pt_b], inc=(kk == 3))
                en = kb.any2()
                dst = nT[:, k0:k0 + 4, s * 128:(s + 1) * 128]
                src = pt[:, 0:512].rearrange("p (a b) -> p a b", b=128)
                if en == "act":
                    kb.op("act", lambda e, dst=dst, src=src: e.activation(out=dst, in_=src, func=AF.Copy),
                          r=[pt_b], w=nT_b[k0:k0 + 4])
                else:
                    kb.op("dve", lambda e, dst=dst, src=src: e.tensor_copy(out=dst, in_=src),
                          r=[pt_b], w=nT_b[k0:k0 + 4])

    eps_t = sb("eps_t", [128, 1])
    eps_q = sb("eps_q", [128, 1])
    kb.op("pool", lambda e: e.memset(eps_t[:], EPS), w=[cb])
    kb.op("pool", lambda e: e.memset(eps_q[:], 128.0 * EPS), w=[cb])

    def ffn(ph_t, kg, ku, kd, xt, xt_b, nT, nT_b, hm, hm_b, wgu, wgu_b, wd, wd_b, sg, sg_b):
        cnt2 = {"i": 0}
        for g in range(22):
            i = g % 2
            kb.dma("sp", wgu[i][:, 0], WGU[kg][g], w=[wgu_b[i]])
            kb.dma("sp", wgu[i][:, 1], WGU[ku][g], w=[wgu_b[i]])
            for c in range(2):
                ffc = g * 2 + c
                pg, pg_b = bank()
                for kc in range(KC):
                    mm(pg[:, 0:TT], wgu[i][:, 0, kc, c * 128:(c + 1) * 128], nT[:, kc, 0:TT],
                       [wgu_b[i], nT_b[kc]], pg_b, start=(kc == 0), stop=(kc == KC - 1))
                pu, pu_b = bank()
                for kc in range(KC):
                    mm(pu[:, 0:TT], wgu[i][:, 1, kc, c * 128:(c + 1) * 128], nT[:, kc, 0:TT],
                       [wgu_b[i], nT_b[kc]], pu_b, start=(kc == 0), stop=(kc == KC - 1))
                j = ffc % 2
                kb.op("act", lambda e, pg=pg, j=j: e.activation(out=sg[j][:, 0:TT], in_=pg[:, 0:TT], func=AF.Silu),
                      r=[pg_b], w=[sg_b[j]])
                kb.op("dve", lambda e, pu=pu, j=j, ffc=ffc: e.tensor_tensor(
                    out=hm[:, ffc, 0:TT], in0=sg[j][:, 0:TT], in1=pu[:, 0:TT], op=ALU.mult),
                    r=[sg_b[j], pu_b], w=[hm_b[ffc]])
        for g in range(4):
            for half in range(2):
                i = half
                wdv = carve(wd[i], 0, [128, NFC // 2, 512], BF16)
                kb.dma("sp", wdv, WDN[kd][g, half], w=[wd_b[i]])
                for s in range(NS):
                    pd, pd_b = psf[s], psf_b[s]
                    for j in range(NFC // 2):
                        ffc = half * (NFC // 2) + j
                        mm(pd[:, 0:512], hm[:, ffc, s * 128:(s + 1) * 128], wdv[:, j, :],
                           [hm_b[ffc], wd_b[i]], pd_b, start=(ffc == 0), stop=(ffc == NFC - 1))
                    if half == 1:
                        kb.op("dve", lambda e, pd=pd, s=s, g=g: e.scalar_tensor_tensor(
                            out=xt[:, s, g * 512:(g + 1) * 512], in0=pd[:, 0:512], scalar=0.5,
                            in1=xt[:, s, g * 512:(g + 1) * 512], op0=ALU.mult, op1=ALU.add),
                            r=[pd_b], w=[xt_b[s]])

    def alloc_tok(ph):
        d = {}
        d["xt"] = sb("xt", [128, NS, D], stack=ph)
        d["xt_b"] = [Buf() for _ in range(NS)]
        d["xn"] = sb("xn", [128, 1, D], BF16, stack=ph)
        xnb = Buf()
        d["xn_b"] = [xnb for _ in range(NS)]
        d["nT"] = sb("nT", [128, KC, TT], BF16, stack=ph)
        d["nT_b"] = [Buf() for _ in range(KC)]
        d["hm"] = sb("hm", [128, NFC, 512], BF16, stack=ph)
        d["hm_b"] = [Buf() for _ in range(NFC)]
        d["wgu"] = [sb("wgu%d" % i, [128, 2, KC, 256], BF16, stack=ph) for i in range(2)]
        d["wgu_b"] = [Buf(), Buf()]
        d["wd"] = [sb("wd%d" % i, [128, NFC, 256], BF16, stack=ph) for i in range(2)]
        d["wd_b"] = [Buf(), Buf()]
        d["sg"] = [sb("sg%d" % i, [128, TT], stack=ph) for i in range(2)]
        d["sg_b"] = [Buf(), Buf()]
        d["ssq"] = sb("ssq", [128, 8], stack=ph)
        d["ssq_b"] = [Buf() for _ in range(8)]
        d["junk"] = d["xn"][:, 0, :]
        d["junk_b"] = xnb
        return d

    def do_norm(d, width=D, nk=KC):
        norm_T(None, d["xt"], d["xt_b"], d["nT"], d["nT_b"], d["xn"], d["xn_b"], d["ssq"], d["ssq_b"],
               d["junk"], d["junk_b"], width, nk)

    def do_ffn(d, kg, ku, kd):
        ffn(None, kg, ku, kd, d["xt"], d["xt_b"], d["nT"], d["nT_b"], d["hm"], d["hm_b"], d["wgu"], d["wgu_b"],
            d["wd"], d["wd_b"], d["sg"], d["sg_b"])

    with ExitStack() as ph:
        d = alloc_tok(ph)
        xt, xt_b, nT, nT_b = d["xt"], d["xt_b"], d["nT"], d["nT_b"]
        hm, hm_b = d["hm"], d["hm_b"]
        wuq = carve(hm, 0, [128, 4, 2048], BF16)
        wukv = carve(hm, 16384, [128, 2, 2048], BF16)
        wkr = carve(hm, 24576, [128, KC, 128], BF16)
        wab = carve(hm, 28672, [128, KC, 32], BF16)
        cs = carve(hm, 29696, [64, 2, TT], F32)
        rtmp = carve(hm, 33792, [64, 2, TT], F32)
        stz0 = carve(hm, 37888, [128, 1056], F32)
        wres_b = Buf()
        cs_b = wres_b
        rtmp_b = Buf()
        stz = [stz0]
        stz_b = [Buf()]
        wcq1 = carve(d["wd"][0], 0, [128, KC, 512], BF16)
        wcq2 = carve(d["wd"][1], 0, [128, KC, 256], BF16)
        cqn = sb("cqn", [128, NS, 768], BF16, stack=ph)
        cqn_b = [Buf() for _ in range(NS)]
        cT = sb("cT", [128, 6, TT], BF16, stack=ph)
        cT_b = [Buf() for _ in range(6)]
        ssq2 = sb("ssq2", [128, 8], stack=ph)
        ssq2_b = [Buf() for _ in range(8)]
        stq = [sb("stq%d" % i, [128, TT], BF16, stack=ph) for i in range(4)]
        stq_b = [Buf() for _ in range(4)]
        stf = d["sg"]
        stf_b = d["sg_b"]
        stv = [sb("stv%d" % i, [128, 1024], BF16, stack=ph) for i in range(2)]
        stv_b = [Buf(), Buf()]
        wz = d["wd"]
        wz_b = d["wd_b"]
        wg = d["wgu"]
        wg_b = d["wgu_b"]
        wcq_b = wz_b
        sq_i = {"q": 0, "f": 0, "v": 0, "z": 0}

        def rope_out(pa, pa_b, pb, pb_b, dst_dram):
            kb.op("dve", lambda e: e.tensor_tensor(out=rtmp[:, 0, :], in0=pa[0:64, 0:TT], in1=cs[:, 0, :], op=ALU.mult),
                  r=[pa_b, cs_b], w=[rtmp_b])
            kb.op("dve", lambda e: e.tensor_tensor(out=rtmp[:, 1, :], in0=pb[0:64, 0:TT], in1=cs[:, 1, :], op=ALU.mult),
                  r=[pb_b, cs_b], w=[rtmp_b])
            i = sq_i["q"] % 4
            sq_i["q"] += 1
            kb.op("dve", lambda e, i=i: e.tensor_tensor(out=stq[i][0:64, :], in0=rtmp[:, 0, :], in1=rtmp[:, 1, :],
                                                        op=ALU.add), r=[rtmp_b], w=[stq_b[i]])
            kb.dma("pool", dst_dram, stq[i][0:64, :], r=[stq_b[i]])

        for t in range(NT):
            t0 = t * TT
            for s in range(NS):
                kb.dma("sp", xt[:, s, :], x_in[t0 + s * 128:t0 + (s + 1) * 128, :], w=[xt_b[s]])
            merge(hm_b, [wres_b, rtmp_b, stz_b[0]])
            do_norm(d)
            do_ffn(d, "g1", "u1", "d1")
            for s in range(NS):
                kb.dma("pool", H1[t0 + s * 128:t0 + (s + 1) * 128, :], xt[:, s, :], r=[xt_b[s]])
            for bb in (wres_b, rtmp_b, stz_b[0]):
                bb.w = {}
                bb.r = {}
            merge([wres_b, rtmp_b, stz_b[0]], hm_b)
            kb.dma("sp", cs[:, 0, :], cos_in[:, t0:t0 + TT], w=[wres_b])
            kb.dma("sp", cs[:, 1, :], sin_in[:, t0:t0 + TT], w=[wres_b])
            kb.dma("sp", wkr, WKR, w=[wres_b])
            kb.dma("sp", wuq, WUQ, w=[wres_b])
            kb.dma("sp", wukv, WUKV, w=[wres_b])
            kb.dma("sp", wab, WAB, w=[wres_b])
            kb.dma("sp", wcq1, WCQ[:, :, 0:512], w=[wcq_b[0]])
            kb.dma("sp", wcq2, WCQ[:, :, 512:768], w=[wcq_b[1]])
            do_norm(d)
            for s in range(NS):
                p1, p1_b = bank()
                for kc in range(KC):
                    mm(p1[:, 0:512], nT[:, kc, s * 128:(s + 1) * 128], wcq1[:, kc, :], [nT_b[kc], wcq_b[0]], p1_b,
                       start=(kc == 0), stop=(kc == KC - 1))
                p2, p2_b = bank()
                for kc in range(KC):
                    mm(p2[:, 0:256], nT[:, kc, s * 128:(s + 1) * 128], wcq2[:, kc, :], [nT_b[kc], wcq_b[1]], p2_b,
                       start=(kc == 0), stop=(kc == KC - 1))
                for (pp_, pp_b, lo, hi, col) in ((p1, p1_b, 0, 512, s), (p2, p2_b, 512, 768, 4 + s)):
                    kb.op("act", lambda e, pp_=pp_, lo=lo, hi=hi, col=col: e.activation(
                        out=d["xn"][:, 0, 0:hi - lo], in_=pp_[:, 0:hi - lo], func=AF.Square,
                        accum_out=ssq2[:, col:col + 1]), r=[pp_b], w=[d["xn_b"][0], ssq2_b[col]])
                    kb.op("act", lambda e, col=col, lo=lo, hi=hi: e.activation(
                        out=ssq2[:, col:col + 1], in_=ssq2[:, col:col + 1], func=AF.Sqrt, scale=1.0 / (hi - lo),
                        bias=eps_t[:, 0:1]), r=[cb], w=[ssq2_b[col]])
                    kb.op("dve", lambda e, col=col: e.reciprocal(out=ssq2[:, col:col + 1], in_=ssq2[:, col:col + 1]),
                          w=[ssq2_b[col]])
                    kb.op("dve", lambda e, s=s, pp_=pp_, lo=lo, hi=hi, col=col: e.tensor_scalar(
                        out=cqn[:, s, lo:hi], in0=pp_[:, 0:hi - lo], scalar1=ssq2[:, col:col + 1], scalar2=None,
                        op0=ALU.mult), r=[pp_b, ssq2_b[col]], w=[cqn_b[s]])
            for kc in range(6):
                pt, pt_b = tbank()
                for s in range(NS):
                    kb.op("pe", lambda e, s=s, kc=kc, pt=pt: e.transpose(
                        out=pt[:, s * 128:(s + 1) * 128], in_=cqn[:, s, kc * 128:(kc + 1) * 128], identity=idb[:]),
                        r=[cqn_b[s], cb], w=[pt_b], inc=(s == NS - 1))
                kb.op("act", lambda e, kc=kc, pt=pt: e.activation(out=cT[:, kc, 0:TT], in_=pt[:, 0:TT], func=AF.Copy),
                      r=[pt_b], w=[cT_b[kc]])
            for h in range(H):
                pq, pq_b = bank()
                for kc in range(4):
                    mm(pq[:, 0:TT], wuq[:, kc, h * 128:(h + 1) * 128], cT[:, kc, 0:TT], [wres_b, cT_b[kc]], pq_b,
                       start=(kc == 0), stop=(kc == 3))
                i = sq_i["q"] % 4
                sq_i["q"] += 1
                kb.op("act", lambda e, pq=pq, i=i: e.activation(out=stq[i][:, :], in_=pq[:, 0:TT], func=AF.Copy),
                      r=[pq_b], w=[stq_b[i]])
                kb.dma("pool", QNT[h, :, t0:t0 + TT], stq[i][:, :], r=[stq_b[i]])
                pa, pa_b = bank()
                for kc in range(4):
                    mm(pa[0:64, 0:TT], wuq[:, kc, 1024 + h * 64:1024 + (h + 1) * 64], cT[:, kc, 0:TT],
                       [wres_b, cT_b[kc]], pa_b, start=(kc == 0), stop=(kc == 3))
                pb, pb_b = bank()
                for kc in range(4):
                    mm(pb[0:64, 0:TT], wuq[:, kc, 1536 + h * 64:1536 + (h + 1) * 64], cT[:, kc, 0:TT],
                       [wres_b, cT_b[kc]], pb_b, start=(kc == 0), stop=(kc == 3))
                rope_out(pa, pa_b, pb, pb_b, QRT[h, :, t0:t0 + TT])
                pk, pk_b = bank()
                for kc in range(2):
                    mm(pk[:, 0:TT], wukv[:, kc, h * 128:(h + 1) * 128], cT[:, 4 + kc, 0:TT], [wres_b, cT_b[4 + kc]],
                       pk_b, start=(kc == 0), stop=(kc == 1))
                i = sq_i["q"] % 4
                sq_i["q"] += 1
                kb.op("dve", lambda e, pk=pk, i=i: e.tensor_copy(out=stq[i][:, :], in_=pk[:, 0:TT]),
                      r=[pk_b], w=[stq_b[i]])
                kb.dma("pool", KNT[h, :, t0:t0 + TT], stq[i][:, :], r=[stq_b[i]])
            for s in range(NS):
                i = sq_i["v"] % 2
                sq_i["v"] += 1
                for half in range(2):
                    pv, pv_b = bank()
                    for kc in range(2):
                        mm(pv[:, 0:512], cT[:, 4 + kc, s * 128:(s + 1) * 128],
                           wukv[:, kc, 1024 + half * 512:1024 + (half + 1) * 512], [wres_b, cT_b[4 + kc]], pv_b,
                           start=(kc == 0), stop=(kc == 1))
                    en = "act" if half else "dve"
                    if en == "act":
                        kb.op("act", lambda e, pv=pv, i=i, half=half: e.activation(
                            out=stv[i][:, half * 512:(half + 1) * 512], in_=pv[:, 0:512], func=AF.Copy),
                            r=[pv_b], w=[stv_b[i]])
                    else:
                        kb.op("dve", lambda e, pv=pv, i=i, half=half: e.tensor_copy(
                            out=stv[i][:, half * 512:(half + 1) * 512], in_=pv[:, 0:512]), r=[pv_b], w=[stv_b[i]])
                kb.dma("pool", VTM[t0 + s * 128:t0 + (s + 1) * 128, :], stv[i][:, :], r=[stv_b[i]])
            pa, pa_b = bank()
            for kc in range(KC):
                mm(pa[0:64, 0:TT], wkr[:, kc, 0:64], nT[:, kc, 0:TT], [wres_b, nT_b[kc]], pa_b,
                   start=(kc == 0), stop=(kc == KC - 1))
            pb, pb_b = bank()
            for kc in range(KC):
                mm(pb[0:64, 0:TT], wkr[:, kc, 64:128], nT[:, kc, 0:TT], [wres_b, nT_b[kc]], pb_b,
                   start=(kc == 0), stop=(kc == KC - 1))
            rope_out(pa, pa_b, pb, pb_b, KRT[:, t0:t0 + TT])
            for g in range(12):
                i = g % 2
                kb.dma("sp", wg[i][:, 0], WGQ[g], w=[wg_b[i]])
                for c in range(2):
                    cc = g * 2 + c
                    pg, pg_b = bank()
                    for kc in range(KC):
                        mm(pg[:, 0:TT], wg[i][:, 0, kc, c * 128:(c + 1) * 128], nT[:, kc, 0:TT],
                           [wg_b[i], nT_b[kc]], pg_b, start=(kc == 0), stop=(kc == KC - 1))
                    j = sq_i["f"] % 2
                    sq_i["f"] += 1
                    if cc % 2:
                        kb.op("act", lambda e, pg=pg, j=j: e.activation(out=stf[j][:, 0:TT], in_=pg[:, 0:TT], func=AF.Copy),
                              r=[pg_b], w=[stf_b[j]])
                    else:
                        kb.op("dve", lambda e, pg=pg, j=j: e.tensor_copy(out=stf[j][:, 0:TT], in_=pg[:, 0:TT]),
                              r=[pg_b], w=[stf_b[j]])
                    kb.dma("pool", GQKV[cc, :, t0:t0 + TT], stf[j][:, 0:TT], r=[stf_b[j]])
            for zi in range(2):
                kb.dma("sp", carve(wz[zi], 0, [128, KC, 512], BF16), WZ[zi], w=[wz_b[zi]])
            for s in range(NS):
                i = 0
                for zi in range(2):
                    wzv = carve(wz[zi], 0, [128, KC, 512], BF16)
                    pz, pz_b = bank()
                    for kc in range(KC):
                        mm(pz[:, 0:512], nT[:, kc, s * 128:(s + 1) * 128], wzv[:, kc, :], [nT_b[kc], wz_b[zi]], pz_b,
                           start=(kc == 0), stop=(kc == KC - 1))
                    if zi:
                        kb.op("act", lambda e, pz=pz, i=i, zi=zi: e.activation(
                            out=stz[i][:, zi * 512:(zi + 1) * 512], in_=pz[:, 0:512], func=AF.Copy),
                            r=[pz_b], w=[stz_b[i]])
                    else:
                        kb.op("dve", lambda e, pz=pz, i=i, zi=zi: e.tensor_copy(
                            out=stz[i][:, zi * 512:(zi + 1) * 512], in_=pz[:, 0:512]), r=[pz_b], w=[stz_b[i]])
                pz, pz_b = bank()
                for kc in range(KC):
                    mm(pz[:, 0:32], nT[:, kc, s * 128:(s + 1) * 128], wab[:, kc, :], [nT_b[kc], wres_b], pz_b,
                       start=(kc == 0), stop=(kc == KC - 1))
                kb.op("dve", lambda e, pz=pz, i=i: e.tensor_copy(out=stz[i][:, 1024:1056], in_=pz[:, 0:32]),
                      r=[pz_b], w=[stz_b[i]])
                kb.dma("pool", ZTM[t0 + s * 128:t0 + (s + 1) * 128, :], stz[i][:, 0:1024], r=[stz_b[i]])
                kb.dma("pool", ABT[t0 + s * 128:t0 + (s + 1) * 128, :], stz[i][:, 1024:1056], r=[stz_b[i]])
        kb.barrier()

    SCALE = float((128 + 64) ** -0.5)
    QT = min(512, T)
    NQT = T // QT
    with ExitStack() as ph:
        km = sb("km", [128, NCH], stack=ph)
        krt = sb("krt", [64, T], BF16, stack=ph)
        res_b = Buf()
        kb.dma("sp", km[:], kmask_in, w=[res_b])
        kb.dma("sp", krt[:], KRT, w=[res_b])
        knt = [sb("knt%d" % i, [128, T], BF16, stack=ph) for i in range(2)]
        qnt = [sb("qnt%d" % i, [128, T], BF16, stack=ph) for i in range(2)]
        qrt = [sb("qrt%d" % i, [64, T], BF16, stack=ph) for i in range(2)]
        vt = [sb("vt%d" % i, [128, NCH, 128], BF16, stack=ph) for i in range(2)]
        hd_b = [Buf(), Buf()]
        pT = [sb("pT%d" % i, [128, QT], BF16, stack=ph) for i in range(3)]
        pT_b = [Buf() for _ in range(3)]
        rden = sb("rden", [128, QT], stack=ph)
        rden_b = Buf()
        dacc = [sb("dacc%d" % i, [128, QT], stack=ph) for i in range(2)]
        dacc_b = [Buf(), Buf()]
        ost = [sb("ost%d" % i, [128, QT], BF16, stack=ph) for i in range(2)]
        ost_b = [Buf(), Buf()]
        n_pt = 0
        n_o = 0
        ps_banks = [(psf[4], psf_b[4]), (psf[5], psf_b[5]),
                    (psb[0][:, :].bitcast(F32), psb_b[0]), (psb[1][:, :].bitcast(F32), psb_b[1])]
        LOOK = 2

        def load_head(h):
            i = h % 2
            kb.dma("sp", knt[i][:], KNT[h], w=[hd_b[i]])
            kb.dma("sp", qnt[i][:], QNT[h], w=[hd_b[i]])
            kb.dma("sp", qrt[i][:], QRT[h], w=[hd_b[i]])
            kb.dma("sp", vt[i][:], VTM[:, h * 128:(h + 1) * 128].rearrange("(c p) d -> p c d", p=128), w=[hd_b[i]])

        def qk_unit(h, qt, kc):
            i = h % 2
            q0 = qt * QT
            ps, ps_b = ps_banks[st_a["pt"] % 4]
            j = st_a["pt"] % 3
            st_a["pt"] += 1
            mm(ps[:, 0:QT], knt[i][:, kc * 128:(kc + 1) * 128], qnt[i][:, q0:q0 + QT], [hd_b[i]], ps_b,
               start=True, stop=False)
            mm(ps[:, 0:QT], krt[:, kc * 128:(kc + 1) * 128], qrt[i][:, q0:q0 + QT], [hd_b[i], res_b], ps_b,
               start=False, stop=True)
            return (h, qt, kc, ps, ps_b, j)

        def pv_unit(u):
            h, qt, kc, ps, ps_b, j = u
            i = h % 2
            q0 = qt * QT
            o = (h * NQT + qt) % 2
            po, po_b = psf[o], psf_b[o]
            pden, pden_b = psf[2 + o], psf_b[2 + o]
            kb.op("act", lambda e: e.activation(out=pT[j][:, :], in_=ps[:, 0:QT], func=AF.Exp, scale=SCALE,
                                                bias=km[:, kc:kc + 1]), r=[ps_b, res_b], w=[pT_b[j]])
            mm(po[:, 0:QT], vt[i][:, kc, :], pT[j][:, :], [hd_b[i], pT_b[j]], po_b,
               start=(kc == 0), stop=(kc == NCH - 1))
            mm(pden[:, 0:QT], onesb[:], pT[j][:, :], [cb, pT_b[j]], pden_b,
               start=(kc == 0), stop=(kc == NCH - 1))
            if kc == NCH - 1:
                kb.op("dve", lambda e: e.reciprocal(out=rden[:, :], in_=pden[:, 0:QT]), r=[pden_b], w=[rden_b])
                jj = o
                kb.op("dve", lambda e: e.tensor_tensor(out=ost[jj][:, :], in0=po[:, 0:QT], in1=rden[:, :],
                                                       op=ALU.mult), r=[po_b, rden_b], w=[ost_b[jj]])
                kb.dma("pool", YT[h, :, q0:q0 + QT], ost[jj][:, :], r=[ost_b[jj]])

        st_a = {"pt": 0}
        pend = []
        load_head(0)
        for h in range(H):
            loaded = False
            for qt in range(NQT):
                for kc in range(NCH):
                    pend.append(qk_unit(h, qt, kc))
                    if len(pend) > LOOK:
                        pv_unit(pend.pop(0))
                    if not loaded and h + 1 < H and all(u[0] == h for u in pend):
                        load_head(h + 1)
                        loaded = True
        while pend:
            pv_unit(pend.pop(0))
        kb.barrier()

    TP = min(2048, T)
    with ExitStack() as ph:
        cw = sb("cw", [128, 24, 5], stack=ph)
        cw_b = Buf()
        for j in range(5):
            kb.dma("sp", cw[:, :, j:j + 1], W["conv_w"][j].rearrange("(c p o) -> p c o", p=128, o=1), w=[cw_b],
                   allow_slow_non_contiguous=True)
        tmk = sb("tmk", [128, T], stack=ph)
        kb.dma("sp", tmk[:], tmask_in.partition_broadcast(128), w=[cw_b])
        xin = [sb("xin%d" % i, [128, TP + 4], stack=ph) for i in range(2)]
        xin_b = [Buf(), Buf()]
        acc = [sb("acc%d" % i, [128, TP], stack=ph) for i in range(2)]
        acc_b = [Buf(), Buf()]
        sq = sb("sq", [128, TP], stack=ph)
        sq_b = Buf()
        rn = sb("rn", [128, TP], stack=ph)
        rn_b = Buf()
        tst = [sb("tst%d" % i, [128, 512], stack=ph) for i in range(2)]
        tst_b = [Buf(), Buf()]
        n_it = 0
        n_ts = 0
        for cc in range(24):
            for pc in range(T // TP):
                c0 = pc * TP
                i = n_it % 2
                n_it += 1
                lo = max(c0 - 2, 0)
                hi = min(c0 + TP + 2, T)
                if lo > c0 - 2:
                    kb.op("pool", lambda e, i=i: e.memset(xin[i][:, 0:2], 0.0), w=[xin_b[i]])
                if hi < c0 + TP + 2:
                    kb.op("pool", lambda e, i=i: e.memset(xin[i][:, TP + 2:TP + 4], 0.0), w=[xin_b[i]])
                kb.dma("sp", xin[i][:, lo - (c0 - 2):hi - (c0 - 2)], GQKV[cc, :, lo:hi], w=[xin_b[i]])
                en = "dve"
                kb.op("act", lambda e, i=i, cc=cc: e.activation(out=acc[i][:, :], in_=xin[i][:, 0:TP], func=AF.Copy,
                                                                scale=cw[:, cc, 0:1]),
                      r=[xin_b[i], cw_b], w=[acc_b[i]])
                for j in range(1, 5):
                    kb.op(en, lambda e, i=i, cc=cc, j=j: e.scalar_tensor_tensor(
                        out=acc[i][:, :], in0=xin[i][:, j:j + TP], scalar=cw[:, cc, j:j + 1], in1=acc[i][:, :],
                        op0=ALU.mult, op1=ALU.add), r=[xin_b[i], cw_b], w=[acc_b[i]])
                kb.op("act", lambda e, i=i: e.activation(out=acc[i][:, :], in_=acc[i][:, :], func=AF.Silu),
                      w=[acc_b[i]])
                kb.op("dve", lambda e, i=i, c0=c0: e.tensor_tensor(out=acc[i][:, :], in0=acc[i][:, :],
                                                                   in1=tmk[:, c0:c0 + TP], op=ALU.mult),
                      r=[cw_b], w=[acc_b[i]])
                if cc < 16:
                    kb.op("act", lambda e, i=i: e.activation(out=sq[:, :], in_=acc[i][:, :], func=AF.Square),
                          r=[acc_b[i]], w=[sq_b])
                    for q in range(TP // min(512, TP)):
                        wd_ = min(512, TP)
                        pss, pss_b = bank()
                        mm(pss[:, 0:wd_], onesf[:], sq[:, q * wd_:(q + 1) * wd_], [cb, sq_b], pss_b)
                        kb.op("act", lambda e, pss=pss, q=q, wd_=wd_: e.activation(
                            out=rn[:, q * wd_:(q + 1) * wd_], in_=pss[:, 0:wd_], func=AF.Sqrt,
                            scale=(128.0 if cc < 8 else 1.0), bias=eps_q[:, 0:1] if cc < 8 else eps_t[:, 0:1]),
                            r=[pss_b, cb], w=[rn_b])
                    kb.op("dve", lambda e: e.reciprocal(out=rn[:, :], in_=rn[:, :]), w=[rn_b])
                    kb.op("dve", lambda e, i=i: e.tensor_tensor(out=acc[i][:, :], in0=acc[i][:, :], in1=rn[:, :],
                                                                op=ALU.mult), r=[rn_b], w=[acc_b[i]])
                    dst = (GQT if cc < 8 else GKT)[cc % 8, :, c0:c0 + TP]
                    kb.dma("pool", dst, acc[i][:, :], r=[acc_b[i]])
                if cc >= 8:
                    hh = cc % 8
                    dstT = GKTM if cc < 16 else GVTM
                    for q in range(TP // 128 // min(4, TP // 128)):
                        nb = min(4, TP // 128)
                        ptb, ptb_b = bank()
                        for b in range(nb):
                            col = (q * nb + b) * 128
                            kb.op("pe", lambda e, ptb=ptb, b=b, col=col, i=i: e.transpose(
                                out=ptb[:, b * 128:(b + 1) * 128], in_=acc[i][:, col:col + 128], identity=idf[:]),
                                r=[acc_b[i], cb], w=[ptb_b], inc=(b == nb - 1))
                        j = n_ts % 2
                        n_ts += 1
                        kb.op("act", lambda e, ptb=ptb, j=j, nb=nb: e.activation(
                            out=tst[j][:, 0:nb * 128], in_=ptb[:, 0:nb * 128], func=AF.Copy), r=[ptb_b], w=[tst_b[j]])
                        r0 = c0 + q * nb * 128
                        kb.dma("pool", dstT[r0:r0 + nb * 128, hh * 128:(hh + 1) * 128].rearrange("(b p) d -> p b d", p=128),
                               tst[j][:, 0:nb * 128].rearrange("p (b d) -> p b d", d=128), r=[tst_b[j]])
        kb.barrier()

    with ExitStack() as ph:
        ab = sb("ab", [128, NCH, 32], stack=ph)
        g_b = Buf()
        kb.dma("sp", ab[:], ABT.rearrange("(c p) n -> p c n", p=128), w=[g_b])
        alog = sb("alog", [128, 16], stack=ph)
        dtb = sb("dtb", [128, 16], stack=ph)
        kb.dma("sp", alog[:], W["a_log"].partition_broadcast(128), w=[g_b])
        kb.dma("sp", dtb[:], W["dt_bias"].partition_broadcast(128), w=[g_b])
        gg = sb("gg", [128, NCH, 16], stack=ph)
        nbeta = sb("nbeta", [128, NCH, 16], stack=ph)
        beta = sb("beta", [128, NCH, 16], stack=ph)
        kb.op("act", lambda e: e.activation(out=alog[:], in_=alog[:], func=AF.Exp), w=[g_b])
        kb.op("dve", lambda e: e.tensor_tensor(out=gg[:], in0=ab[:, :, 0:16],
                                               in1=dtb[:].unsqueeze(1).to_broadcast([128, NCH, 16]), op=ALU.add),
              w=[g_b])
        kb.op("act", lambda e: e.activation(out=gg[:], in_=gg[:], func=AF.Exp), w=[g_b])
        kb.op("act", lambda e: e.activation(out=gg[:], in_=gg[:], func=AF.Ln, bias=1.0), w=[g_b])
        kb.op("dve", lambda e: e.scalar_tensor_tensor(out=gg[:], in0=gg[:], scalar=-1.0,
                                                      in1=alog[:].unsqueeze(1).to_broadcast([128, NCH, 16]),
                                                      op0=ALU.mult, op1=ALU.mult), w=[g_b])
        kb.op("act", lambda e: e.activation(out=beta[:], in_=ab[:, :, 16:32], func=AF.Exp, scale=-1.0), w=[g_b])
        kb.op("dve", lambda e: e.tensor_scalar(out=beta[:], in0=beta[:], scalar1=1.0, scalar2=None, op0=ALU.add),
              w=[g_b])
        kb.op("dve", lambda e: e.reciprocal(out=beta[:], in_=beta[:]), w=[g_b])
        kb.op("dve", lambda e: e.tensor_scalar(out=nbeta[:], in0=beta[:], scalar1=-1.0, scalar2=None, op0=ALU.mult),
              w=[g_b])
        eGi = sb("eGi", [128, 2, NCH, 8], stack=ph)
        neGi = sb("neGi", [128, 2, NCH, 8], stack=ph)
        eGr = sb("eGr", [128, 2, NCH, 8], stack=ph)
        eGt = sb("eGt", [128, 2, NCH, 8], stack=ph)
        CW_ = min(NCH, 64)
        for dr in range(2):
            for c0 in range(0, NCH, CW_):
                for (dst, msk) in ((eGi, (m_le, m_ge)[dr]), (eGr, (m_gt, m_lt)[dr]), (eGt, onesf)):
                    pc_, pc_b = bank()
                    mm(pc_[:, 0:CW_ * 8].rearrange("p (c h) -> p c h", h=8), msk[:], gg[:, c0:c0 + CW_, dr * 8:dr * 8 + 8],
                       [cb, g_b], pc_b)
                    kb.op("act", lambda e, dst=dst, pc_=pc_, dr=dr, c0=c0: e.activation(
                        out=dst[:, dr, c0:c0 + CW_, :], in_=pc_[:, 0:CW_ * 8].rearrange("p (c h) -> p c h", h=8),
                        func=AF.Exp), r=[pc_b], w=[g_b])
        kb.op("dve", lambda e: e.tensor_scalar(out=neGi[:], in0=eGi[:], scalar1=-1.0, scalar2=None, op0=ALU.mult),
              w=[g_b])

        S = sb("S", [128, 16, 128], stack=ph)
        S_b = [Buf() for _ in range(4)]
        kb.op("pool", lambda e: e.memset(S[:], 0.0), w=S_b)
        identb = idf
        qT = [[sb("gq%d%d" % (a, b), [128, 8, 128], stack=ph) for b in range(2)] for a in range(2)]
        kT = [[sb("gk%d%d" % (a, b), [128, 8, 128], stack=ph) for b in range(2)] for a in range(2)]
        ktm = [[sb("gkm%d%d" % (a, b), [128, 8, 128], stack=ph) for b in range(2)] for a in range(2)]
        vtm = [[sb("gvm%d%d" % (a, b), [128, 8, 128], stack=ph) for b in range(2)] for a in range(2)]
        ld_b = [[Buf(), Buf()], [Buf(), Buf()]]

        def wtile(name):
            return sb(name, [128, 4, 128], stack=ph), Buf()

        GR = []
        for dr in range(2):
            for hg in range(2):
                G = {"dr": dr, "hg": hg, "hs": [hg * 4 + x for x in range(4)], "sb": S_b[dr * 2 + hg]}
                for nm in ("P", "Q", "R2", "DS", "LTa", "LTb", "La", "Lb", "R", "qkd"):
                    G[nm], G[nm + "_b"] = wtile("%s%d%d" % (nm, dr, hg))
                GR.append(G)

        def load_step(st):
            for dr in range(2):
                c = st if dr == 0 else NCH - 1 - st
                bi = st % 2
                tb = ld_b[dr][bi]
                sl = slice(c * 128, (c + 1) * 128)
                kb.dma("sp", qT[dr][bi][:], GQT[:, :, sl].rearrange("h p t -> p h t"), w=[tb])
                kb.dma("sp", kT[dr][bi][:], GKT[:, :, sl].rearrange("h p t -> p h t"), w=[tb])
                kb.dma("sp", ktm[dr][bi][:], GKTM[sl, :].rearrange("p (h d) -> p h d", d=128), w=[tb])
                kb.dma("sp", vtm[dr][bi][:], GVTM[sl, :].rearrange("p (h d) -> p h d", d=128), w=[tb])

        b4 = lambda t: t[:].unsqueeze(1).to_broadcast([128, 4, 128])
        r32 = lambda ap: ap.bitcast(F32R)
        m_le_r = sb("m_le_r", [128, 128], stack=ph)
        m_ge_r = sb("m_ge_r", [128, 128], stack=ph)
        kb.op("dve", lambda e: e.tensor_copy(out=r32(m_le_r[:]), in_=m_le[:]), r=[cb], w=[cb])
        kb.op("dve", lambda e: e.tensor_copy(out=r32(m_ge_r[:]), in_=m_ge[:]), r=[cb], w=[cb])
        kb.op("dve", lambda e: e.tensor_copy(out=r32(S[:]), in_=S[:]), w=S_b)

        def mmr(out, lhsT, rhs, rbufs, wbuf):
            kb.op("pe", lambda e: e.matmul(out=out, lhsT=r32(lhsT), rhs=r32(rhs), start=True, stop=True),
                  r=rbufs, w=[wbuf], inc=True)

        def round_loads(st):
            for dr in range(2):
                bi = st % 2
                tb = ld_b[dr][bi]
                kb.op("dve", lambda e: e.tensor_copy(out=r32(kT[dr][bi][:]), in_=kT[dr][bi][:]), w=[tb])
                kb.op("act", lambda e: e.activation(out=r32(qT[dr][bi][:]), in_=qT[dr][bi][:], func=AF.Copy), w=[tb])
                kb.op("dve", lambda e: e.tensor_copy(out=r32(ktm[dr][bi][:]), in_=ktm[dr][bi][:]), w=[tb])
        fl = lambda t: t[:].rearrange("p a b -> p (a b)")

        def ctx(G, st):
            dr = G["dr"]
            G["c"] = st if dr == 0 else NCH - 1 - st
            G["bi"] = st % 2
            G["tb"] = ld_b[dr][st % 2]
            G["m_dm_l"] = (m_gt, m_lt)[dr]
            G["m_dm_r"] = (m_le_r, m_ge_r)[dr]
            G["m_incl"] = (m_le, m_ge)[dr]
            G["m_str"] = (m_lt, m_gt)[dr]

        def s1a(G):
            dr, c = G["dr"], G["c"]
            for x, h in enumerate(G["hs"]):
                kb.op("act", lambda e, x=x, h=h: e.activation(
                    out=r32(G["P"][:, x, :]), in_=G["m_dm_l"][:], func=AF.Copy,
                    scale=gg[:, c, dr * 8 + h:dr * 8 + h + 1]), r=[cb, g_b], w=[G["P_b"]])
            G["pdm"], G["pdm_b"] = bank()
            for x in range(4):
                mmr(G["pdm"][:, x * 128:(x + 1) * 128], G["P"][:, x, :], G["m_dm_r"][:], [G["P_b"], cb], G["pdm_b"])

        def s1b(G):
            kb.op("act", lambda e: e.activation(out=r32(fl(G["Q"])), in_=G["pdm"][:, :], func=AF.Exp),
                  r=[G["pdm_b"]], w=[G["Q_b"]])
            kb.op("dve", lambda e: e.tensor_tensor(out=r32(G["R2"][:]), in0=G["Q"][:], in1=b4(G["m_incl"]), op=ALU.mult),
                  r=[G["Q_b"], cb], w=[G["R2_b"]])
            kb.op("dve", lambda e: e.tensor_tensor(out=G["DS"][:], in0=G["Q"][:], in1=b4(G["m_str"]), op=ALU.mult),
                  r=[G["Q_b"], cb], w=[G["DS_b"]])

        def s2a(G):
            dr, bi, tb = G["dr"], G["bi"], G["tb"]
            G["pkk"], G["pkk_b"] = bank()
            for x, h in enumerate(G["hs"]):
                mmr(G["pkk"][:, x * 128:(x + 1) * 128], kT[dr][bi][:, h, :], kT[dr][bi][:, h, :], [tb], G["pkk_b"])

        def s2a2(G):
            dr, bi, tb = G["dr"], G["bi"], G["tb"]
            G["pqk"], G["pqk_b"] = bank()
            for x, h in enumerate(G["hs"]):
                mmr(G["pqk"][:, x * 128:(x + 1) * 128], kT[dr][bi][:, h, :], qT[dr][bi][:, h, :], [tb], G["pqk_b"])

        def s2b(G):
            dr, c = G["dr"], G["c"]
            kb.op("dve", lambda e: e.tensor_tensor(out=r32(fl(G["LTa"])), in0=G["pkk"][:, :], in1=fl(G["DS"]), op=ALU.mult),
                  r=[G["pkk_b"], G["DS_b"]], w=[G["LTa_b"]])
            for x, h in enumerate(G["hs"]):
                kb.op("act", lambda e, x=x, h=h: e.activation(
                    out=r32(G["LTa"][:, x, :]), in_=G["LTa"][:, x, :], func=AF.Copy,
                    scale=nbeta[:, c, dr * 8 + h:dr * 8 + h + 1]), r=[g_b], w=[G["LTa_b"]])

        def s2b2(G):
            kb.op("dve", lambda e: e.tensor_tensor(out=r32(fl(G["qkd"])), in0=G["pqk"][:, :], in1=fl(G["R2"]), op=ALU.mult),
                  r=[G["pqk_b"], G["R2_b"]], w=[G["qkd_b"]])

        def s3a(G):
            G["ptr"], G["ptr_b"] = bank()
            for x in range(4):
                kb.op("pe", lambda e, x=x: e.transpose(out=G["ptr"][:, x * 128:(x + 1) * 128], in_=G["LTa"][:, x, :],
                                                       identity=idf[:]),
                      r=[G["LTa_b"], cb], w=[G["ptr_b"]], inc=(x == 3))

        def s3b(G):
            kb.op("act", lambda e: e.activation(out=r32(fl(G["La"])), in_=G["ptr"][:, :], func=AF.Copy),
                  r=[G["ptr_b"]], w=[G["La_b"]])
            kb.op("dve", lambda e: e.tensor_tensor(out=r32(G["R"][:]), in0=G["LTa"][:], in1=b4(idf), op=ALU.add),
                  r=[G["LTa_b"], cb], w=[G["R_b"]])
            G["cur"] = "a"

        def lev_a(G, last):
            cur = G["cur"]
            L_, L_b_, LT_, LT_b_ = G["L" + cur], G["L" + cur + "_b"], G["LT" + cur], G["LT" + cur + "_b"]
            G["pL"], G["pL_b"] = bank()
            for x in range(4):
                mmr(G["pL"][:, x * 128:(x + 1) * 128], LT_[:, x, :], L_[:, x, :], [LT_b_, L_b_], G["pL_b"])

        def lev_b(G, last):
            nx = "b" if G["cur"] == "a" else "a"
            kb.op("act", lambda e: e.activation(out=r32(fl(G["L" + nx])), in_=G["pL"][:, :], func=AF.Copy),
                  r=[G["pL_b"]], w=[G["L" + nx + "_b"]])

        def lev_a2(G, last):
            cur = G["cur"]
            L_, L_b_, LT_, LT_b_ = G["L" + cur], G["L" + cur + "_b"], G["LT" + cur], G["LT" + cur + "_b"]
            if not last:
                G["pLT"], G["pLT_b"] = bank()
                for x in range(4):
                    mmr(G["pLT"][:, x * 128:(x + 1) * 128], L_[:, x, :], LT_[:, x, :], [LT_b_, L_b_], G["pLT_b"])

        def lev_b2(G, last):
            nx = "b" if G["cur"] == "a" else "a"
            if not last:
                kb.op("dve", lambda e: e.tensor_copy(out=r32(fl(G["LT" + nx])), in_=G["pLT"][:, :]),
                      r=[G["pLT_b"]], w=[G["LT" + nx + "_b"]])
            G["cur"] = nx

        def lev_c(G, last):
            cur = G["cur"]
            G["pR"], G["pR_b"] = bank()
            for x in range(4):
                mmr(G["pR"][:, x * 128:(x + 1) * 128], G["L" + cur][:, x, :], G["R"][:, x, :],
                   [G["L" + cur + "_b"], G["R_b"]], G["pR_b"])

        def lev_d(G, last):
            kb.op("dve", lambda e: e.tensor_tensor(out=r32(fl(G["R"])), in0=G["pR"][:, :], in1=fl(G["R"]), op=ALU.add),
                  r=[G["pR_b"]], w=[G["R_b"]])

        def s5a(G):
            dr, bi, tb = G["dr"], G["bi"], G["tb"]
            G["pks"], G["pks_b"] = bank()
            for x, h in enumerate(G["hs"]):
                mmr(G["pks"][:, x * 128:(x + 1) * 128], kT[dr][bi][:, h, :], S[:, dr * 8 + h, :], [tb, G["sb"]], G["pks_b"])

        def s5a2(G):
            dr, bi, tb = G["dr"], G["bi"], G["tb"]
            G["pqs"], G["pqs_b"] = bank()
            for x, h in enumerate(G["hs"]):
                mmr(G["pqs"][:, x * 128:(x + 1) * 128], qT[dr][bi][:, h, :], S[:, dr * 8 + h, :], [tb, G["sb"]], G["pqs_b"])

        def s5b(G):
            dr, bi, tb, c = G["dr"], G["bi"], G["tb"], G["c"]
            for x, h in enumerate(G["hs"]):
                kb.op("dve", lambda e, x=x, h=h: e.scalar_tensor_tensor(
                    out=r32(G["P"][:, x, :]), in0=G["pks"][:, x * 128:(x + 1) * 128], scalar=neGi[:, dr, c, h:h + 1],
                    in1=vtm[dr][bi][:, h, :], op0=ALU.mult, op1=ALU.add), r=[G["pks_b"], g_b, tb], w=[G["P_b"]])

        def s5b2(G):
            kb.op("act", lambda e: e.activation(out=fl(G["DS"]), in_=G["pqs"][:, :], func=AF.Copy),
                  r=[G["pqs_b"]], w=[G["DS_b"]])

        def s5c(G):
            G["pvn"], G["pvn_b"] = bank()
            for x in range(4):
                mmr(G["pvn"][:, x * 128:(x + 1) * 128], G["R"][:, x, :], G["P"][:, x, :], [G["R_b"], G["P_b"]], G["pvn_b"])

        def s5d(G):
            dr, c = G["dr"], G["c"]
            for x, h in enumerate(G["hs"]):
                kb.op("act", lambda e, x=x, h=h: e.activation(
                    out=r32(G["Q"][:, x, :]), in_=G["pvn"][:, x * 128:(x + 1) * 128], func=AF.Copy,
                    scale=beta[:, c, dr * 8 + h:dr * 8 + h + 1]), r=[G["pvn_b"], g_b], w=[G["Q_b"]])
            for x, h in enumerate(G["hs"]):
                kb.op("act", lambda e, x=x, h=h: e.activation(
                    out=r32(G["R2"][:, x, :]), in_=G["Q"][:, x, :], func=AF.Copy, scale=eGr[:, dr, c, h:h + 1]),
                    r=[G["Q_b"], g_b], w=[G["R2_b"]])

        def s5e(G):
            dr, bi, tb = G["dr"], G["bi"], G["tb"]
            G["po"], G["po_b"] = bank()
            for x in range(4):
                mmr(G["po"][:, x * 128:(x + 1) * 128], G["qkd"][:, x, :], G["Q"][:, x, :], [G["qkd_b"], G["Q_b"]], G["po_b"])

        def s5e2(G):
            dr, bi, tb = G["dr"], G["bi"], G["tb"]
            G["pds"], G["pds_b"] = bank()
            for x, h in enumerate(G["hs"]):
                mmr(G["pds"][:, x * 128:(x + 1) * 128], ktm[dr][bi][:, h, :], G["R2"][:, x, :], [tb, G["R2_b"]], G["pds_b"])

        def s5f2(G):
            dr, c, hg = G["dr"], G["c"], G["hg"]
            for x, h in enumerate(G["hs"]):
                kb.op("dve", lambda e, x=x, h=h: e.scalar_tensor_tensor(
                    out=r32(S[:, dr * 8 + h, :]), in0=S[:, dr * 8 + h, :], scalar=eGt[:, dr, c, h:h + 1],
                    in1=G["pds"][:, x * 128:(x + 1) * 128], op0=ALU.mult, op1=ALU.add), r=[G["pds_b"], g_b], w=[G["sb"]])

        def s5f(G):
            dr, c, hg = G["dr"], G["c"], G["hg"]
            for x, h in enumerate(G["hs"]):
                kb.op("dve", lambda e, x=x, h=h: e.scalar_tensor_tensor(
                    out=G["DS"][:, x, :], in0=G["DS"][:, x, :], scalar=eGi[:, dr, c, h:h + 1],
                    in1=G["po"][:, x * 128:(x + 1) * 128], op0=ALU.mult, op1=ALU.add), r=[G["po_b"], g_b], w=[G["DS_b"]])
            kb.dma("sp", (OF, OB)[dr][c * 128:(c + 1) * 128, hg * 512:(hg + 1) * 512], fl(G["DS"]), r=[G["DS_b"]])

        stages = [s1a, s1b, s2a, s2b, s2a2, s2b2, s3a, s3b]
        for lev in range(6):
            last = (lev == 5)
            for f_ in (lev_a, lev_b, lev_a2, lev_b2, lev_c, lev_d):
                stages.append(lambda G, f_=f_, last=last: f_(G, last))
        stages += [s5a, s5b, s5a2, s5b2, s5c, s5d, s5e, s5f, s5e2, s5f2]

        load_step(0)
        for st in range(NCH):
            if st + 1 < NCH:
                load_step(st + 1)
            round_loads(st)
            for G in GR:
                ctx(G, st)
            for stg in stages:
                for G in GR:
                    stg(G)
        kb.barrier()

    with ExitStack() as ph:
        gdn_g = sb("gdn_g", [128, 128], stack=ph)
        c_b = Buf()
        kb.dma("sp", gdn_g[:], W["gdn_norm"].partition_broadcast(128), w=[c_b])
        of_ = [sb("of%d" % i, [128, 1024], stack=ph) for i in range(2)]
        ob_ = [sb("ob%d" % i, [128, 1024], stack=ph) for i in range(2)]
        zz = [sb("zz%d" % i, [128, 1024], stack=ph) for i in range(2)]
        o_b = [Buf(), Buf()]
        ob16 = [sb("ob16%d" % i, [128, 1024], BF16, stack=ph) for i in range(2)]
        ob16_b = [Buf(), Buf()]
        yst = [sb("yst%d" % i, [128, 8, 128], BF16, stack=ph) for i in range(2)]
        yst_b = [Buf(), Buf()]
        ss3 = sb("ss3", [128, 16], stack=ph)
        ss3_b = [Buf(), Buf()]
        hd3 = lambda ap: ap.rearrange("p (h d) -> p h d", d=128)
        for c in range(NCH):
            i = c % 2
            r0 = c * 128
            kb.dma("sp", of_[i][:], OF[r0:r0 + 128, :], w=[o_b[i]])
            kb.dma("sp", ob_[i][:], OB[r0:r0 + 128, :], w=[o_b[i]])
            kb.dma("sp", zz[i][:], ZTM[r0:r0 + 128, :], w=[o_b[i]])
            sv = ss3[:, i * 8:(i + 1) * 8]
            kb.op("dve", lambda e: e.tensor_tensor(out=of_[i][:], in0=of_[i][:], in1=ob_[i][:], op=ALU.add), w=[o_b[i]])
            kb.op("act", lambda e: e.activation(out=ob_[i][:], in_=of_[i][:], func=AF.Square), w=[o_b[i]])
            kb.op("dve", lambda e: e.tensor_reduce(out=sv, in_=hd3(ob_[i][:]), axis=mybir.AxisListType.X, op=ALU.add),
                  r=[o_b[i]], w=[ss3_b[i]])
            kb.op("act", lambda e: e.activation(out=sv, in_=sv, func=AF.Sqrt, scale=1.0 / 128, bias=eps_t[:, 0:1]),
                  r=[cb], w=[ss3_b[i]])
            kb.op("dve", lambda e: e.reciprocal(out=sv, in_=sv), w=[ss3_b[i]])
            kb.op("act", lambda e: e.activation(out=zz[i][:], in_=zz[i][:], func=AF.Silu), w=[o_b[i]])
            kb.op("dve", lambda e: e.tensor_tensor(out=hd3(of_[i][:]), in0=hd3(of_[i][:]),
                                                   in1=sv.unsqueeze(2).to_broadcast([128, 8, 128]), op=ALU.mult),
                  r=[ss3_b[i]], w=[o_b[i]])
            kb.op("dve", lambda e: e.tensor_tensor(out=hd3(zz[i][:]), in0=hd3(zz[i][:]),
                                                   in1=gdn_g[:].unsqueeze(1).to_broadcast([128, 8, 128]), op=ALU.mult),
                  r=[c_b], w=[o_b[i]])
            kb.op("dve", lambda e: e.tensor_tensor(out=ob16[i][:], in0=of_[i][:], in1=zz[i][:], op=ALU.mult),
                  r=[o_b[i]], w=[ob16_b[i]])
            for half in range(2):
                ptb, ptb_b = tbank()
                for q in range(4):
                    hh = half * 4 + q
                    kb.op("pe", lambda e, q=q, hh=hh: e.transpose(
                        out=ptb[:, q * 128:(q + 1) * 128], in_=ob16[i][:, hh * 128:(hh + 1) * 128], identity=idb[:]),
                        r=[ob16_b[i], cb], w=[ptb_b], inc=(q == 3))
                kb.op("act", lambda e, half=half: e.activation(
                    out=yst[i][:, half * 4:(half + 1) * 4, :], in_=ptb[:, 0:512].rearrange("p (a b) -> p a b", b=128),
                    func=AF.Copy), r=[ptb_b], w=[yst_b[i]])
            kb.dma("sp", YT[8:16, :, r0:r0 + 128].rearrange("h p t -> p h t"), yst[i][:], r=[yst_b[i]])
        kb.barrier()

    with ExitStack() as ph:
        d = alloc_tok(ph)
        xt, xt_b, nT, nT_b = d["xt"], d["xt_b"], d["nT"], d["nT_b"]
        gfin = sb("gfin", [128, D], stack=ph)
        c_b = Buf()
        kb.dma("sp", gfin[:], W["norm_final"].partition_broadcast(128), w=[c_b])
        hm, hm_b = d["hm"], d["hm_b"]
        yT = carve(hm, 0, [128, KC, TT], BF16)
        yT_b = [Buf() for _ in range(KC)]
        of_ = carve(hm, 16384, [128, 1024], F32)
        ob_ = carve(hm, 20480, [128, 1024], F32)
        zz = carve(hm, 24576, [128, 1024], F32)
        o_b = Buf()
        ob16 = carve(hm, 28672, [128, NS, 1024], BF16)
        ob16_b = [Buf() for _ in range(NS)]
        wple = carve(hm, 0, [128, 2, 2048], BF16)
        pt_ = carve(hm, 8192, [128, NS, PLE], F32)
        pb16 = carve(hm, 12288, [128, NS, PLE], BF16)
        pT = carve(hm, 14336, [128, 2, TT], BF16)
        gate = [carve(hm, 16384 + i * 2048, [128, 512], F32) for i in range(2)]
        pt_b = Buf()
        pb16_b = Buf()
        pT_b = Buf()
        gate_b = [Buf(), Buf()]
        ss3 = sb("ss3", [128, 8], stack=ph)
        ss3_b = Buf()
        pre_bufs = yT_b + [o_b] + ob16_b
        post_bufs = [pt_b, pb16_b, pT_b] + gate_b
        wo = d["wd"]
        wo_b = d["wd_b"]
        n_g = 0
        for t in range(NT):
            t0 = t * TT
            for s in range(NS):
                kb.dma("sp", xt[:, s, :], H1[t0 + s * 128:t0 + (s + 1) * 128, :], w=[xt_b[s]])
            for bb in pre_bufs:
                bb.w = {}
                bb.r = {}
            merge(pre_bufs, hm_b + post_bufs)
            kb.dma("sp", yT[:, 0:8, :], YT[0:8, :, t0:t0 + TT].rearrange("h p t -> p h t"), w=yT_b[0:8])
            kb.dma("sp", yT[:, 8:16, :], YT[8:16, :, t0:t0 + TT].rearrange("h p t -> p h t"), w=yT_b[8:16])
            for g in range(4):
                i = g % 2
                wov = carve(wo[i], 0, [128, KC, 512], BF16)
                kb.dma("sp", wov, WOUT[g], w=[wo_b[i]])
                for s in range(NS):
                    pd, pd_b = bank()
                    for kc in range(KC):
                        mm(pd[:, 0:512], yT[:, kc, s * 128:(s + 1) * 128], wov[:, kc, :], [yT_b[kc], wo_b[i]], pd_b,
                           start=(kc == 0), stop=(kc == KC - 1))
                    kb.op("dve", lambda e, pd=pd, s=s, g=g: e.tensor_tensor(
                        out=xt[:, s, g * 512:(g + 1) * 512], in0=pd[:, 0:512], in1=xt[:, s, g * 512:(g + 1) * 512],
                        op=ALU.add), r=[pd_b], w=[xt_b[s]])
            do_norm(d)
            merge(hm_b, pre_bufs)
            do_ffn(d, "g2", "u2", "d2")
            for bb in post_bufs:
                bb.w = {}
                bb.r = {}
            merge(post_bufs, hm_b)
            kb.dma("sp", wple, WPLE, w=[pt_b])
            kb.dma("sp", pt_, p_in[t0:t0 + TT, :].rearrange("(s p) d -> p s d", p=128), w=[pt_b])
            do_norm(d)
            kb.op("dve", lambda e: e.tensor_copy(out=pb16, in_=pt_), r=[pt_b], w=[pb16_b])
            for kc in range(2):
                ptb, ptb_b = tbank()
                for s in range(NS):
                    kb.op("pe", lambda e, ptb=ptb, s=s, kc=kc: e.transpose(
                        out=ptb[:, s * 128:(s + 1) * 128], in_=pb16[:, s, kc * 128:(kc + 1) * 128], identity=idb[:]),
                        r=[pb16_b, cb], w=[ptb_b], inc=(s == NS - 1))
                kb.op("act", lambda e, ptb=ptb, kc=kc: e.activation(out=pT[:, kc, 0:TT], in_=ptb[:, 0:TT], func=AF.Copy),
                      r=[ptb_b], w=[pT_b])
            for g in range(4):
                i = g % 2
                wov = carve(wo[i], 0, [128, KC, 512], BF16)
                kb.dma("sp", wov, WPG[g], w=[wo_b[i]])
                for s in range(NS):
                    pg, pg_b = bank()
                    for kc in range(KC):
                        mm(pg[:, 0:512], nT[:, kc, s * 128:(s + 1) * 128], wov[:, kc, :], [nT_b[kc], wo_b[i]], pg_b,
                           start=(kc == 0), stop=(kc == KC - 1))
                    pp, pp_b = bank()
                    for kc in range(2):
                        mm(pp[:, 0:512], pT[:, kc, s * 128:(s + 1) * 128], wple[:, kc, g * 512:(g + 1) * 512],
                           [pT_b, pt_b], pp_b, start=(kc == 0), stop=(kc == 1))
                    j = n_g % 2
                    n_g += 1
                    kb.op("act", lambda e, pg=pg, j=j: e.activation(out=gate[j], in_=pg[:, 0:512], func=AF.Sigmoid),
                          r=[pg_b], w=[gate_b[j]])
                    kb.op("dve", lambda e, pp=pp, j=j: e.tensor_tensor(out=gate[j], in0=pp[:, 0:512], in1=gate[j],
                                                                       op=ALU.mult), r=[pp_b], w=[gate_b[j]])
                    kb.op("dve", lambda e, j=j, s=s, g=g: e.tensor_tensor(
                        out=xt[:, s, g * 512:(g + 1) * 512], in0=xt[:, s, g * 512:(g + 1) * 512], in1=gate[j],
                        op=ALU.add), r=[gate_b[j]], w=[xt_b[s]])
            for s in range(NS):
                kb.op("act", lambda e, s=s: e.activation(out=d["junk"][:, 0:D], in_=xt[:, s, :], func=AF.Square,
                                                         accum_out=d["ssq"][:, s:s + 1]),
                      r=[xt_b[s]], w=[d["junk_b"], d["ssq_b"][s]])
                kb.op("act", lambda e, s=s: e.activation(out=d["ssq"][:, s:s + 1], in_=d["ssq"][:, s:s + 1],
                                                         func=AF.Sqrt, scale=1.0 / D, bias=eps_t[:, 0:1]),
                      r=[cb], w=[d["ssq_b"][s]])
                kb.op("dve", lambda e, s=s: e.reciprocal(out=d["ssq"][:, s:s + 1], in_=d["ssq"][:, s:s + 1]),
                      w=[d["ssq_b"][s]])
                kb.op("dve", lambda e, s=s: e.scalar_tensor_tensor(
                    out=xt[:, s, :], in0=xt[:, s, :], scalar=d["ssq"][:, s:s + 1], in1=gfin[:], op0=ALU.mult,
                    op1=ALU.mult), r=[d["ssq_b"][s], c_b], w=[xt_b[s]])
            for s in range(NS):
                kb.dma("pool", y_out[t0 + s * 128:t0 + (s + 1) * 128, :], xt[:, s, :], r=[xt_b[s]])
        kb.barrier()
    return nc, es


def _rope_tables(T):
    inv = 1.0 / (10000.0 ** (np.arange(0, 64, 2, dtype=np.float32) / 64.0))
    ang = np.arange(T, dtype=np.float32)[:, None] * inv[None, :].astype(np.float32)
    c = np.cos(ang).astype(np.float32).T
    s = np.sin(ang).astype(np.float32).T
    return (np.ascontiguousarray(np.concatenate([c, c], 0)),
            np.ascontiguousarray(np.concatenate([-s, s], 0)))


_CACHE = {}


def core_inputs(xs, ps, Tpad, weights):
    S = xs.shape[0]
    x = np.zeros((Tpad, D), np.float32)
    x[:S] = xs
    p = np.zeros((Tpad, PLE), np.float32)
    p[:S] = ps
    km = np.zeros((Tpad,), np.float32)
    km[S:] = -60.0
    km = np.ascontiguousarray(km.reshape(Tpad // 128, 128).T)
    c2, s2 = _rope_tables(Tpad)
    tm = np.zeros((Tpad,), np.float32)
    tm[:S] = 1.0
    m = {"x": x, "p": p, "kmask": km, "tmask": tm, "cos2": c2, "sin2": s2}
    m.update(weights)
    return m


def kernel(**inputs):
    inp = {k: np.asarray(v) for k, v in inputs.items()}
    B, S, _ = inp["x_prompt"].shape
    DB, DS, _ = inp["x_sample"].shape
    T = S
    weights = {}
    for k, v in inp.items():
        if k in ("x_prompt", "x_sample", "p_prompt", "p_sample"):
            continue
        a = np.ascontiguousarray(v, dtype=np.float32)
        if k != "norm_final":
            a = a[0]
        if k in ("a_log", "dt_bias"):
            a = np.ascontiguousarray(a.reshape(16))
        weights[k] = a
    in_maps = []
    for i in range(B):
        in_maps.append(core_inputs(inp["x_prompt"][i], inp["p_prompt"][0, i], T, weights))
    for i in range(DB):
        in_maps.append(core_inputs(inp["x_sample"][i], inp["p_sample"][0, i], T, weights))
    key = (T,)
    if key not in _CACHE:
        _CACHE[key] = build(T, min(512, T))
    nc, _es = _CACHE[key]
    res = run_bass_kernel_spmd(nc, in_maps, core_ids=list(range(len(in_maps))))
    yp = np.stack([np.asarray(res.results[i]["y"])[:S] for i in range(B)]).astype(np.float32)
    ys = np.stack([np.asarray(res.results[B + i]["y"])[:DS] for i in range(DB)]).astype(np.float32)
    return (yp, ys)
```

```python
import numpy as np
from contextlib import ExitStack
import concourse.bass as bass
import concourse.mybir as mybir
from concourse.bass_utils import run_bass_kernel_spmd

F32 = mybir.dt.float32
BF16 = mybir.dt.bfloat16
F32R = mybir.dt.float32r
AF = mybir.ActivationFunctionType
ALU = mybir.AluOpType

D = 2048
FF = 5632
KC = D // 128
NFC = FF // 128
PLE = 256
H = 8
EPS = 1e-6
NDS = 24
IN_DIM = 4960


class Buf:
    __slots__ = ("w", "r")

    def __init__(self):
        self.w = {}
        self.r = {}


class KB:
    def __init__(self, nc, es):
        self.nc = nc
        self.E = {}
        for name, eng in (("pe", nc.tensor), ("act", nc.scalar), ("dve", nc.vector),
                          ("pool", nc.gpsimd), ("sp", nc.sync)):
            self.E[name] = dict(eng=eng, sem=es.enter_context(nc.semaphore("s_" + name)), cnt=0, seen={})
        self.dsems = [es.enter_context(nc.semaphore("d%d" % i)) for i in range(NDS)]
        self.dtot = [0] * NDS
        self.dnext = 0
        self.dnext_p = 0
        self.rr = 0

    def _sem(self, key):
        return self.E[key]["sem"] if isinstance(key, str) else self.dsems[key[1]]

    def wait(self, en, key, val):
        E = self.E[en]
        if E["seen"].get(key, 0) >= val:
            return
        E["eng"].wait_ge(self._sem(key), val)
        E["seen"][key] = val

    def _deps(self, en, r, w):
        toks = {}
        for b in r:
            for k, v in b.w.items():
                if toks.get(k, 0) < v:
                    toks[k] = v
        for b in w:
            for k, v in b.w.items():
                if toks.get(k, 0) < v:
                    toks[k] = v
            for k, v in b.r.items():
                if toks.get(k, 0) < v:
                    toks[k] = v
        for k, v in toks.items():
            if k == en and en == "pe":
                continue
            self.wait(en, k, v)

    def _mark(self, tok, r, w):
        k, v = tok
        for b in r:
            if b.r.get(k, 0) < v:
                b.r[k] = v
        for b in w:
            if b.r:
                b.w = {k: v}
                b.r = {}
            else:
                b.w[k] = v

    def op(self, en, fn, r=(), w=(), inc=True):
        E = self.E[en]
        self._deps(en, r, w)
        inst = fn(E["eng"])
        tok = (en, E["cnt"] + 1)
        if inc:
            E["cnt"] += 1
            inst.then_inc(E["sem"], 1)
        self._mark(tok, r, w)
        return inst

    def dma(self, qn, out, in_, r=(), w=(), **kw):
        E = self.E[qn]
        self._deps(qn, r, w)
        if qn == "pool":
            i = NDS - 8 + self.dnext_p
            self.dnext_p = (self.dnext_p + 1) % 8
        else:
            i = self.dnext
            self.dnext = (i + 1) % (NDS - 8)
        if self.dtot[i] > 0:
            self.wait(qn, ("d", i), self.dtot[i])
        inst = E["eng"].dma_start(out=out, in_=in_, **kw)
        self.dtot[i] += 16
        inst.then_inc(self.dsems[i], 16)
        self._mark((("d", i), self.dtot[i]), r, w)

    def barrier(self):
        for en in self.E:
            for k2, E2 in self.E.items():
                if k2 != en and E2["cnt"] > 0:
                    self.wait(en, k2, E2["cnt"])
            for i in range(NDS):
                if self.dtot[i] > 0:
                    self.wait(en, ("d", i), self.dtot[i])

    def any3(self):
        self.rr = (self.rr + 1) % 3
        return ("dve", "pool", "act")[self.rr]

    def any2(self):
        self.rr = (self.rr + 1) % 2
        return ("dve", "act")[self.rr]


def carve(t, off, shape, dt):
    flat = t[:].rearrange("p a b -> p (a b)")
    nel = 1
    for v in shape[1:]:
        nel *= v
    esz = 4 if dt == F32 else 2
    v = flat[0:shape[0], off // 2: off // 2 + nel * esz // 2]
    if dt == F32:
        v = v.bitcast(F32)
    if len(shape) == 3:
        v = v.rearrange("p (a b) -> p a b", b=shape[2])
    return v


def merge(dsts, srcs):
    for dd in dsts:
        for ss in srcs:
            for src in (ss.w, ss.r):
                for k, v in src.items():
                    if dd.w.get(k, 0) < v:
                        dd.w[k] = v


def scale_cast(kb, en, out, in_, scal):
    if en == "act":
        kb_fn = lambda e: e.activation(out=out, in_=in_, func=AF.Copy, scale=scal)
    else:
        kb_fn = lambda e: e.tensor_scalar(out=out, in0=in_, scalar1=scal, scalar2=None, op0=ALU.mult)
    return kb_fn


def build(T, TT):
    NT = T // TT
    NS = TT // 128
    NCH = T // 128
    nc = bass.Bass("TRN2", target_bir_lowering=False)
    es = ExitStack()
    kb = KB(nc, es)

    def din(name, shape, dt=F32):
        return nc.dram_tensor(name, list(shape), dt, kind="ExternalInput").ap()

    def dscr(name, shape, dt=BF16):
        return nc.dram_tensor(name, list(shape), dt, kind="Internal").ap()

    x_in = din("x", [T, D])
    p_in = din("p", [T, PLE])
    kmask_in = din("kmask", [128, NCH])
    tmask_in = din("tmask", [T])
    cos_in = din("cos2", [64, T])
    sin_in = din("sin2", [64, T])
    W = {}
    for nm, shp in (("norm_ffn1", [D]), ("ffn1_w_gate", [D, FF]), ("ffn1_w_up", [D, FF]), ("ffn1_w_down", [FF, D]),
                    ("norm_mix", [D]), ("w_in", [D, IN_DIM]), ("q_norm", [512]), ("w_uq", [512, 1536]),
                    ("kv_norm", [256]), ("w_ukv", [256, 2048]), ("conv_w", [5, 3072]), ("a_log", [16]),
                    ("dt_bias", [16]), ("gdn_norm", [128]), ("w_out", [D, D]), ("norm_ffn2", [D]),
                    ("ffn2_w_gate", [D, FF]), ("ffn2_w_up", [D, FF]), ("ffn2_w_down", [FF, D]),
                    ("norm_ple", [D]), ("w_ple", [PLE, D]), ("w_ple_gate", [D, D]), ("norm_final", [D])):
        W[nm] = din(nm, shp)
    y_out = nc.dram_tensor("y", [T, D], F32, kind="ExternalOutput").ap()

    WGU = {k: dscr("s_" + k, [22, 128, KC, 256]) for k in ("g1", "u1", "g2", "u2")}
    WDN = {k: dscr("s_" + k, [4, 2, 128, NFC // 2, 512]) for k in ("d1", "d2")}
    WCQ = dscr("s_wcq", [128, KC, 768])
    WKR = dscr("s_wkr", [128, KC, 128])
    WGQ = dscr("s_wgq", [12, 128, KC, 256])
    WZ = dscr("s_wz", [2, 128, KC, 512])
    WAB = dscr("s_wab", [128, KC, 32])
    WUQ = dscr("s_wuq", [128, 4, 2048])
    WUKV = dscr("s_wukv", [128, 2, 2048])
    WOUT = dscr("s_wout", [4, 128, KC, 512])
    WPG = dscr("s_wpg", [4, 128, KC, 512])
    WPLE = dscr("s_wple", [128, 2, 2048])
    H1 = dscr("s_h1", [T, D], F32)
    QNT = dscr("s_qnt", [H, 128, T])
    QRT = dscr("s_qrt", [H, 64, T])
    KNT = dscr("s_knt", [H, 128, T])
    KRT = dscr("s_krt", [64, T])
    VTM = dscr("s_vtm", [T, H * 128])
    GQKV = dscr("s_gqkv", [24, 128, T], F32)
    ZTM = dscr("s_ztm", [T, 1024], F32)
    ABT = dscr("s_abt", [T, 32], F32)
    YT = dscr("s_yt", [KC, 128, T])
    GQT = dscr("s_gqt", [H, 128, T], F32)
    GKT = dscr("s_gkt", [H, 128, T], F32)
    GKTM = dscr("s_gktm", [T, H * 128], F32)
    GVTM = dscr("s_gvtm", [T, H * 128], F32)
    OF = dscr("s_of", [T, H * 128], F32)
    OB = dscr("s_ob", [T, H * 128], F32)

    uniq = {"n": 0}

    def sb(name, shape, dt=F32, stack=es):
        uniq["n"] += 1
        return stack.enter_context(nc.sbuf_tensor("%s_%d" % (name, uniq["n"]), list(shape), dt))

    idf = sb("idf", [128, 128])
    idb = sb("idb", [128, 128], BF16)
    onesb = sb("onesb", [128, 128], BF16)
    onesf = sb("onesf", [128, 128])
    m_le = sb("m_le", [128, 128])
    m_lt = sb("m_lt", [128, 128])
    m_ge = sb("m_ge", [128, 128])
    m_gt = sb("m_gt", [128, 128])
    cb = Buf()
    psf = [es.enter_context(nc.psum_tensor("psf%d" % i, [128, 512], F32)) for i in range(6)]
    psb = [es.enter_context(nc.psum_tensor("psb%d" % i, [128, 1024], BF16)) for i in range(2)]
    psf_b = [Buf() for _ in range(6)]
    psb_b = [Buf(), Buf()]
    state = {"bank": 0, "tb": 0}

    def bank():
        i = state["bank"]
        state["bank"] = (i + 1) % 6
        assert (not psf_b[i].w) or psf_b[i].r, "PSUM bank handed out again before its consumer was emitted"
        return psf[i], psf_b[i]

    def tbank():
        i = state["tb"]
        state["tb"] = 1 - i
        return psb[i][:, 0:512], psb_b[i]

    kb.op("pool", lambda e: e.memset(onesf[:], 1.0), w=[cb])
    for t, pat, cm, cmp_ in ((idf, [[-1, 128]], 1, ALU.is_equal), (m_le, [[1, 128]], -1, ALU.is_ge),
                             (m_lt, [[1, 128]], -1, ALU.is_gt), (m_ge, [[-1, 128]], 1, ALU.is_ge),
                             (m_gt, [[-1, 128]], 1, ALU.is_gt)):
        kb.op("pool", lambda e, t=t, pat=pat, cm=cm, cmp_=cmp_: e.affine_select(
            out=t[:], in_=onesf[:], pattern=pat, compare_op=cmp_, fill=0.0, base=0, channel_multiplier=cm),
            r=[cb], w=[cb])
    kb.op("dve", lambda e: e.tensor_copy(out=idb[:], in_=idf[:]), r=[cb], w=[cb])
    kb.op("dve", lambda e: e.tensor_copy(out=onesb[:], in_=onesf[:]), r=[cb], w=[cb])

    def mm(out, lhsT, rhs, rbufs, wbuf, start=True, stop=True):
        kb.op("pe", lambda e: e.matmul(out=out, lhsT=lhsT, rhs=rhs, start=start, stop=stop),
              r=rbufs, w=[wbuf], inc=stop)

    def rstd_of(ss, n, stack_tmp, name):
        pass

    with ExitStack() as ph:
        gains = {}
        gb = Buf()
        for nm, k in (("norm_ffn1", KC), ("norm_mix", KC), ("norm_ffn2", KC), ("norm_ple", KC),
                      ("q_norm", 4), ("kv_norm", 2)):
            g = sb("g_" + nm, [128, k], stack=ph)
            kb.dma("sp", g[:], W[nm].rearrange("(k p) -> p k", p=128), w=[gb], allow_slow_non_contiguous=True)
            gains[nm] = g
        NSTG = 2
        stg_f = [sb("stgf%d" % i, [128, FF], stack=ph) for i in range(NSTG)]
        stg_b = [sb("stgb%d" % i, [128, FF], BF16, stack=ph) for i in range(NSTG)]
        stg_fb = [Buf() for _ in range(NSTG)]
        stg_bb = [Buf() for _ in range(NSTG)]
        cnt = {"i": 0}

        def prep(src, nrows, ncols, gain, stores, neg_swap=None):
            for kc in range(nrows // 128):
                i = cnt["i"] % NSTG
                cnt["i"] += 1
                kb.dma("sp", stg_f[i][:, 0:ncols], src[kc * 128:(kc + 1) * 128, :], w=[stg_fb[i]])
                scal = gains[gain][:, kc:kc + 1] if gain else 1.0
                en = kb.any2()
                kb.op(en, scale_cast(kb, en, stg_b[i][:, 0:ncols], stg_f[i][:, 0:ncols], scal),
                      r=[stg_fb[i], gb], w=[stg_bb[i]])
                for dst, view in stores(kc, stg_b[i]):
                    kb.dma("pool", dst, view, r=[stg_bb[i]])

        for k, nm, gn in (("g1", "ffn1_w_gate", "norm_ffn1"), ("u1", "ffn1_w_up", "norm_ffn1"),
                          ("g2", "ffn2_w_gate", "norm_ffn2"), ("u2", "ffn2_w_up", "norm_ffn2")):
            dst = WGU[k].rearrange("g p k c -> p k g c")
            prep(W[nm], D, FF, gn,
                 lambda kc, s, dst=dst: [(dst[:, kc], s[:, 0:FF].rearrange("p (g c) -> p g c", c=256))])
        for k, nm in (("d1", "ffn1_w_down"), ("d2", "ffn2_w_down")):
            prep(W[nm], FF, D, None,
                 lambda kc, s, k=k: [(WDN[k][:, kc // 22, :, kc % 22, :].rearrange("g p c -> p g c"),
                                      s[:, 0:D].rearrange("p (g c) -> p g c", c=512))])
        wgq_v = WGQ.rearrange("g p k c -> p k g c")
        wz_v = WZ.rearrange("g p k c -> p k g c")

        def win_stores(kc, s):
            return [(WCQ[:, kc, :], s[:, 0:768]),
                    (WKR[:, kc, 0:64], s[:, 768:832]),
                    (WKR[:, kc, 64:96], s[:, 800:832]),
                    (WKR[:, kc, 96:128], s[:, 768:800]),
                    (wgq_v[:, kc], s[:, 832:3904].rearrange("p (g c) -> p g c", c=256)),
                    (wz_v[:, kc], s[:, 3904:4928].rearrange("p (g c) -> p g c", c=512)),
                    (WAB[:, kc, :], s[:, 4928:4960])]
        prep(W["w_in"], D, IN_DIM, "norm_mix", win_stores)

        def wuq_stores(kc, s):
            sv = s[:, 0:1536].rearrange("p (h c) -> p h c", c=192)
            return [(WUQ[:, kc, 0:1024].rearrange("p (h c) -> p h c", c=128), sv[:, :, 0:128]),
                    (WUQ[:, kc, 1024:1536].rearrange("p (h c) -> p h c", c=64), sv[:, :, 128:192]),
                    (WUQ[:, kc, 1536:2048].rearrange("p (h c) -> p h c", c=64)[:, :, 0:32], sv[:, :, 160:192]),
                    (WUQ[:, kc, 1536:2048].rearrange("p (h c) -> p h c", c=64)[:, :, 32:64], sv[:, :, 128:160])]
        prep(W["w_uq"], 512, 1536, "q_norm", wuq_stores)

        def wukv_stores(kc, s):
            sv = s[:, 0:2048].rearrange("p (h c) -> p h c", c=256)
            return [(WUKV[:, kc, 0:1024].rearrange("p (h c) -> p h c", c=128), sv[:, :, 0:128]),
                    (WUKV[:, kc, 1024:2048].rearrange("p (h c) -> p h c", c=128), sv[:, :, 128:256])]
        prep(W["w_ukv"], 256, 2048, "kv_norm", wukv_stores)
        for dstT, nm, gn in ((WOUT, "w_out", None), (WPG, "w_ple_gate", "norm_ple")):
            dst = dstT.rearrange("g p k c -> p k g c")
            prep(W[nm], D, D, gn,
                 lambda kc, s, dst=dst: [(dst[:, kc], s[:, 0:D].rearrange("p (g c) -> p g c", c=512))])
        prep(W["w_ple"], PLE, D, None, lambda kc, s: [(WPLE[:, kc, :], s[:, 0:D])])
        kb.barrier()

    def norm_T(ph_t, xt, xt_b, nT, nT_b, xn, xn_b, ssq, ssq_b, junk, junk_b, width, nk, s_list=None):
        for s in range(NS):
            kb.op("act", lambda e, s=s: e.activation(out=xn[:, 0, 0:width], in_=xt[:, s, 0:width], func=AF.Square,
                                                     accum_out=ssq[:, s:s + 1]),
                  r=[xt_b[s]], w=[xn_b[0], ssq_b[s]])
            kb.op("act", lambda e, s=s: e.activation(out=ssq[:, s:s + 1], in_=ssq[:, s:s + 1], func=AF.Sqrt,
                                                     scale=1.0 / width, bias=eps_t[:, 0:1]),
                  r=[cb], w=[ssq_b[s]])
            kb.op("dve", lambda e, s=s: e.reciprocal(out=ssq[:, s:s + 1], in_=ssq[:, s:s + 1]), w=[ssq_b[s]])
            en = "dve" if s % 2 else "act"
            kb.op(en, scale_cast(kb, en, xn[:, 0, 0:width], xt[:, s, 0:width], ssq[:, s:s + 1]),
                  r=[xt_b[s], ssq_b[s]], w=[xn_b[0]])
            for k0 in range(0, nk, 4):
                pt, pt_b = tbank()
                for kk in range(4):
                    kc = k0 + kk
                    kb.op("pe", lambda e, kk=kk, kc=kc, pt=pt: e.transpose(
                        out=pt[:, kk * 128:(kk + 1) * 128], in_=xn[:, 0, kc * 128:(kc + 1) * 128], identity=idb[:]),
                        r=[xn_b[0], cb], w=[pt_b], inc=(kk == 3))
                en = kb.any2()
                dst = nT[:, k0:k0 + 4, s * 128:(s + 1) * 128]
                src = pt[:, 0:512].rearrange("p (a b) -> p a b", b=128)
                if en == "act":
                    kb.op("act", lambda e, dst=dst, src=src: e.activation(out=dst, in_=src, func=AF.Copy),
                          r=[pt_b], w=nT_b[k0:k0 + 4])
                else:
                    kb.op("dve", lambda e, dst=dst, src=src: e.tensor_copy(out=dst, in_=src),
                          r=[pt_b], w=nT_b[k0:k0 + 4])

    eps_t = sb("eps_t", [128, 1])
    eps_q = sb("eps_q", [128, 1])
    kb.op("pool", lambda e: e.memset(eps_t[:], EPS), w=[cb])
    kb.op("pool", lambda e: e.memset(eps_q[:], 128.0 * EPS), w=[cb])

    def ffn(ph_t, kg, ku, kd, xt, xt_b, nT, nT_b, hm, hm_b, wgu, wgu_b, wd, wd_b, sg, sg_b):
        cnt2 = {"i": 0}
        for g in range(22):
            i = g % 2
            kb.dma("sp", wgu[i][:, 0], WGU[kg][g], w=[wgu_b[i]])
            kb.dma("sp", wgu[i][:, 1], WGU[ku][g], w=[wgu_b[i]])
            for c in range(2):
                ffc = g * 2 + c
                pg, pg_b = bank()
                for kc in range(KC):
                    mm(pg[:, 0:TT], wgu[i][:, 0, kc, c * 128:(c + 1) * 128], nT[:, kc, 0:TT],
                       [wgu_b[i], nT_b[kc]], pg_b, start=(kc == 0), stop=(kc == KC - 1))
                pu, pu_b = bank()
                for kc in range(KC):
                    mm(pu[:, 0:TT], wgu[i][:, 1, kc, c * 128:(c + 1) * 128], nT[:, kc, 0:TT],
                       [wgu_b[i], nT_b[kc]], pu_b, start=(kc == 0), stop=(kc == KC - 1))
                j = ffc % 2
                kb.op("act", lambda e, pg=pg, j=j: e.activation(out=sg[j][:, 0:TT], in_=pg[:, 0:TT], func=AF.Silu),
                      r=[pg_b], w=[sg_b[j]])
                kb.op("dve", lambda e, pu=pu, j=j, ffc=ffc: e.tensor_tensor(
                    out=hm[:, ffc, 0:TT], in0=sg[j][:, 0:TT], in1=pu[:, 0:TT], op=ALU.mult),
                    r=[sg_b[j], pu_b], w=[hm_b[ffc]])
        for g in range(4):
            for half in range(2):
                i = half
                wdv = carve(wd[i], 0, [128, NFC // 2, 512], BF16)
                kb.dma("sp", wdv, WDN[kd][g, half], w=[wd_b[i]])
                for s in range(NS):
                    pd, pd_b = psf[s], psf_b[s]
                    for j in range(NFC // 2):
                        ffc = half * (NFC // 2) + j
                        mm(pd[:, 0:512], hm[:, ffc, s * 128:(s + 1) * 128], wdv[:, j, :],
                           [hm_b[ffc], wd_b[i]], pd_b, start=(ffc == 0), stop=(ffc == NFC - 1))
                    if half == 1:
                        kb.op("dve", lambda e, pd=pd, s=s, g=g: e.scalar_tensor_tensor(
                            out=xt[:, s, g * 512:(g + 1) * 512], in0=pd[:, 0:512], scalar=0.5,
                            in1=xt[:, s, g * 512:(g + 1) * 512], op0=ALU.mult, op1=ALU.add),
                            r=[pd_b], w=[xt_b[s]])

    def alloc_tok(ph):
        d = {}
        d["xt"] = sb("xt", [128, NS, D], stack=ph)
        d["xt_b"] = [Buf() for _ in range(NS)]
        d["xn"] = sb("xn", [128, 1, D], BF16, stack=ph)
        xnb = Buf()
        d["xn_b"] = [xnb for _ in range(NS)]
        d["nT"] = sb("nT", [128, KC, TT], BF16, stack=ph)
        d["nT_b"] = [Buf() for _ in range(KC)]
        d["hm"] = sb("hm", [128, NFC, 512], BF16, stack=ph)
        d["hm_b"] = [Buf() for _ in range(NFC)]
        d["wgu"] = [sb("wgu%d" % i, [128, 2, KC, 256], BF16, stack=ph) for i in range(2)]
        d["wgu_b"] = [Buf(), Buf()]
        d["wd"] = [sb("wd%d" % i, [128, NFC, 256], BF16, stack=ph) for i in range(2)]
        d["wd_b"] = [Buf(), Buf()]
        d["sg"] = [sb("sg%d" % i, [128, TT], stack=ph) for i in range(2)]
        d["sg_b"] = [Buf(), Buf()]
        d["ssq"] = sb("ssq", [128, 8], stack=ph)
        d["ssq_b"] = [Buf() for _ in range(8)]
        d["junk"] = d["xn"][:, 0, :]
        d["junk_b"] = xnb
        return d

    def do_norm(d, width=D, nk=KC):
        norm_T(None, d["xt"], d["xt_b"], d["nT"], d["nT_b"], d["xn"], d["xn_b"], d["ssq"], d["ssq_b"],
               d["junk"], d["junk_b"], width, nk)

    def do_ffn(d, kg, ku, kd):
        ffn(None, kg, ku, kd, d["xt"], d["xt_b"], d["nT"], d["nT_b"], d["hm"], d["hm_b"], d["wgu"], d["wgu_b"],
            d["wd"], d["wd_b"], d["sg"], d["sg_b"])

    with ExitStack() as ph:
        d = alloc_tok(ph)
        xt, xt_b, nT, nT_b = d["xt"], d["xt_b"], d["nT"], d["nT_b"]
        hm, hm_b = d["hm"], d["hm_b"]
        wuq = carve(hm, 0, [128, 4, 2048], BF16)
        wukv = carve(hm, 16384, [128, 2, 2048], BF16)
        wkr = carve(hm, 24576, [128, KC, 128], BF16)
        wab = carve(hm, 28672, [128, KC, 32], BF16)
        cs = carve(hm, 29696, [64, 2, TT], F32)
        rtmp = carve(hm, 33792, [64, 2, TT], F32)
        stz0 = carve(hm, 37888, [128, 1056], F32)
        wres_b = Buf()
        cs_b = wres_b
        rtmp_b = Buf()
        stz = [stz0]
        stz_b = [Buf()]
        wcq1 = carve(d["wd"][0], 0, [128, KC, 512], BF16)
        wcq2 = carve(d["wd"][1], 0, [128, KC, 256], BF16)
        cqn = sb("cqn", [128, NS, 768], BF16, stack=ph)
        cqn_b = [Buf() for _ in range(NS)]
        cT = sb("cT", [128, 6, TT], BF16, stack=ph)
        cT_b = [Buf() for _ in range(6)]
        ssq2 = sb("ssq2", [128, 8], stack=ph)
        ssq2_b = [Buf() for _ in range(8)]
        stq = [sb("stq%d" % i, [128, TT], BF16, stack=ph) for i in range(4)]
        stq_b = [Buf() for _ in range(4)]
        stf = d["sg"]
        stf_b = d["sg_b"]
        stv = [sb("stv%d" % i, [128, 1024], BF16, stack=ph) for i in range(2)]
        stv_b = [Buf(), Buf()]
        wz = d["wd"]
        wz_b = d["wd_b"]
        wg = d["wgu"]
        wg_b = d["wgu_b"]
        wcq_b = wz_b
        sq_i = {"q": 0, "f": 0, "v": 0, "z": 0}

        def rope_out(pa, pa_b, pb, pb_b, dst_dram):
            kb.op("dve", lambda e: e.tensor_tensor(out=rtmp[:, 0, :], in0=pa[0:64, 0:TT], in1=cs[:, 0, :], op=ALU.mult),
                  r=[pa_b, cs_b], w=[rtmp_b])
            kb.op("dve", lambda e: e.tensor_tensor(out=rtmp[:, 1, :], in0=pb[0:64, 0:TT], in1=cs[:, 1, :], op=ALU.mult),
                  r=[pb_b, cs_b], w=[rtmp_b])
            i = sq_i["q"] % 4
            sq_i["q"] += 1
            kb.op("dve", lambda e, i=i: e.tensor_tensor(out=stq[i][0:64, :], in0=rtmp[:, 0, :], in1=rtmp[:, 1, :],
                                                        op=ALU.add), r=[rtmp_b], w=[stq_b[i]])
            kb.dma("pool", dst_dram, stq[i][0:64, :], r=[stq_b[i]])

        for t in range(NT):
            t0 = t * TT
            for s in range(NS):
                kb.dma("sp", xt[:, s, :], x_in[t0 + s * 128:t0 + (s + 1) * 128, :], w=[xt_b[s]])
            merge(hm_b, [wres_b, rtmp_b, stz_b[0]])
            do_norm(d)
            do_ffn(d, "g1", "u1", "d1")
            for s in range(NS):
                kb.dma("pool", H1[t0 + s * 128:t0 + (s + 1) * 128, :], xt[:, s, :], r=[xt_b[s]])
            for bb in (wres_b, rtmp_b, stz_b[0]):
                bb.w = {}
                bb.r = {}
            merge([wres_b, rtmp_b, stz_b[0]], hm_b)
            kb.dma("sp", cs[:, 0, :], cos_in[:, t0:t0 + TT], w=[wres_b])
            kb.dma("sp", cs[:, 1, :], sin_in[:, t0:t0 + TT], w=[wres_b])
            kb.dma("sp", wkr, WKR, w=[wres_b])
            kb.dma("sp", wuq, WUQ, w=[wres_b])
            kb.dma("sp", wukv, WUKV, w=[wres_b])
            kb.dma("sp", wab, WAB, w=[wres_b])
            kb.dma("sp", wcq1, WCQ[:, :, 0:512], w=[wcq_b[0]])
            kb.dma("sp", wcq2, WCQ[:, :, 512:768], w=[wcq_b[1]])
            do_norm(d)
            for s in range(NS):
                p1, p1_b = bank()
                for kc in range(KC):
                    mm(p1[:, 0:512], nT[:, kc, s * 128:(s + 1) * 128], wcq1[:, kc, :], [nT_b[kc], wcq_b[0]], p1_b,
                       start=(kc == 0), stop=(kc == KC - 1))
                p2, p2_b = bank()
                for kc in range(KC):
                    mm(p2[:, 0:256], nT[:, kc, s * 128:(s + 1) * 128], wcq2[:, kc, :], [nT_b[kc], wcq_b[1]], p2_b,
                       start=(kc == 0), stop=(kc == KC - 1))
                for (pp_, pp_b, lo, hi, col) in ((p1, p1_b, 0, 512, s), (p2, p2_b, 512, 768, 4 + s)):
                    kb.op("act", lambda e, pp_=pp_, lo=lo, hi=hi, col=col: e.activation(
                        out=d["xn"][:, 0, 0:hi - lo], in_=pp_[:, 0:hi - lo], func=AF.Square,
                        accum_out=ssq2[:, col:col + 1]), r=[pp_b], w=[d["xn_b"][0], ssq2_b[col]])
                    kb.op("act", lambda e, col=col, lo=lo, hi=hi: e.activation(
                        out=ssq2[:, col:col + 1], in_=ssq2[:, col:col + 1], func=AF.Sqrt, scale=1.0 / (hi - lo),
                        bias=eps_t[:, 0:1]), r=[cb], w=[ssq2_b[col]])
                    kb.op("dve", lambda e, col=col: e.reciprocal(out=ssq2[:, col:col + 1], in_=ssq2[:, col:col + 1]),
                          w=[ssq2_b[col]])
                    kb.op("dve", lambda e, s=s, pp_=pp_, lo=lo, hi=hi, col=col: e.tensor_scalar(
                        out=cqn[:, s, lo:hi], in0=pp_[:, 0:hi - lo], scalar1=ssq2[:, col:col + 1], scalar2=None,
                        op0=ALU.mult), r=[pp_b, ssq2_b[col]], w=[cqn_b[s]])
            for kc in range(6):
                pt, pt_b = tbank()
                for s in range(NS):
                    kb.op("pe", lambda e, s=s, kc=kc, pt=pt: e.transpose(
                        out=pt[:, s * 128:(s + 1) * 128], in_=cqn[:, s, kc * 128:(kc + 1) * 128], identity=idb[:]),
                        r=[cqn_b[s], cb], w=[pt_b], inc=(s == NS - 1))
                kb.op("act", lambda e, kc=kc, pt=pt: e.activation(out=cT[:, kc, 0:TT], in_=pt[:, 0:TT], func=AF.Copy),
                      r=[pt_b], w=[cT_b[kc]])
            for h in range(H):
                pq, pq_b = bank()
                for kc in range(4):
                    mm(pq[:, 0:TT], wuq[:, kc, h * 128:(h + 1) * 128], cT[:, kc, 0:TT], [wres_b, cT_b[kc]], pq_b,
                       start=(kc == 0), stop=(kc == 3))
                i = sq_i["q"] % 4
                sq_i["q"] += 1
                kb.op("act", lambda e, pq=pq, i=i: e.activation(out=stq[i][:, :], in_=pq[:, 0:TT], func=AF.Copy),
                      r=[pq_b], w=[stq_b[i]])
                kb.dma("pool", QNT[h, :, t0:t0 + TT], stq[i][:, :], r=[stq_b[i]])
                pa, pa_b = bank()
                for kc in range(4):
                    mm(pa[0:64, 0:TT], wuq[:, kc, 1024 + h * 64:1024 + (h + 1) * 64], cT[:, kc, 0:TT],
                       [wres_b, cT_b[kc]], pa_b, start=(kc == 0), stop=(kc == 3))
                pb, pb_b = bank()
                for kc in range(4):
                    mm(pb[0:64, 0:TT], wuq[:, kc, 1536 + h * 64:1536 + (h + 1) * 64], cT[:, kc, 0:TT],
                       [wres_b, cT_b[kc]], pb_b, start=(kc == 0), stop=(kc == 3))
                rope_out(pa, pa_b, pb, pb_b, QRT[h, :, t0:t0 + TT])
                pk, pk_b = bank()
                for kc in range(2):
                    mm(pk[:, 0:TT], wukv[:, kc, h * 128:(h + 1) * 128], cT[:, 4 + kc, 0:TT], [wres_b, cT_b[4 + kc]],
                       pk_b, start=(kc == 0), stop=(kc == 1))
                i = sq_i["q"] % 4
                sq_i["q"] += 1
                kb.op("dve", lambda e, pk=pk, i=i: e.tensor_copy(out=stq[i][:, :], in_=pk[:, 0:TT]),
                      r=[pk_b], w=[stq_b[i]])
                kb.dma("pool", KNT[h, :, t0:t0 + TT], stq[i][:, :], r=[stq_b[i]])
            for s in range(NS):
                i = sq_i["v"] % 2
                sq_i["v"] += 1
                for half in range(2):
                    pv, pv_b = bank()
                    for kc in range(2):
                        mm(pv[:, 0:512], cT[:, 4 + kc, s * 128:(s + 1) * 128],
                           wukv[:, kc, 1024 + half * 512:1024 + (half + 1) * 512], [wres_b, cT_b[4 + kc]], pv_b,
                           start=(kc == 0), stop=(kc == 1))
                    en = "act" if half else "dve"
                    if en == "act":
                        kb.op("act", lambda e, pv=pv, i=i, half=half: e.activation(
                            out=stv[i][:, half * 512:(half + 1) * 512], in_=pv[:, 0:512], func=AF.Copy),
                            r=[pv_b], w=[stv_b[i]])
                    else:
                        kb.op("dve", lambda e, pv=pv, i=i, half=half: e.tensor_copy(
                            out=stv[i][:, half * 512:(half + 1) * 512], in_=pv[:, 0:512]), r=[pv_b], w=[stv_b[i]])
                kb.dma("pool", VTM[t0 + s * 128:t0 + (s + 1) * 128, :], stv[i][:, :], r=[stv_b[i]])
            pa, pa_b = bank()
            for kc in range(KC):
                mm(pa[0:64, 0:TT], wkr[:, kc, 0:64], nT[:, kc, 0:TT], [wres_b, nT_b[kc]], pa_b,
                   start=(kc == 0), stop=(kc == KC - 1))
            pb, pb_b = bank()
            for kc in range(KC):
                mm(pb[0:64, 0:TT], wkr[:, kc, 64:128], nT[:, kc, 0:TT], [wres_b, nT_b[kc]], pb_b,
                   start=(kc == 0), stop=(kc == KC - 1))
            rope_out(pa, pa_b, pb, pb_b, KRT[:, t0:t0 + TT])
            for g in range(12):
                i = g % 2
                kb.dma("sp", wg[i][:, 0], WGQ[g], w=[wg_b[i]])
                for c in range(2):
                    cc = g * 2 + c
                    pg, pg_b = bank()
                    for kc in range(KC):
                        mm(pg[:, 0:TT], wg[i][:, 0, kc, c * 128:(c + 1) * 128], nT[:, kc, 0:TT],
                           [wg_b[i], nT_b[kc]], pg_b, start=(kc == 0), stop=(kc == KC - 1))
                    j = sq_i["f"] % 2
                    sq_i["f"] += 1
                    if cc % 2:
                        kb.op("act", lambda e, pg=pg, j=j: e.activation(out=stf[j][:, 0:TT], in_=pg[:, 0:TT], func=AF.Copy),
                              r=[pg_b], w=[stf_b[j]])
                    else:
                        kb.op("dve", lambda e, pg=pg, j=j: e.tensor_copy(out=stf[j][:, 0:TT], in_=pg[:, 0:TT]),
                              r=[pg_b], w=[stf_b[j]])
                    kb.dma("pool", GQKV[cc, :, t0:t0 + TT], stf[j][:, 0:TT], r=[stf_b[j]])
            for zi in range(2):
                kb.dma("sp", carve(wz[zi], 0, [128, KC, 512], BF16), WZ[zi], w=[wz_b[zi]])
            for s in range(NS):
                i = 0
                for zi in range(2):
                    wzv = carve(wz[zi], 0, [128, KC, 512], BF16)
                    pz, pz_b = bank()
                    for kc in range(KC):
                        mm(pz[:, 0:512], nT[:, kc, s * 128:(s + 1) * 128], wzv[:, kc, :], [nT_b[kc], wz_b[zi]], pz_b,
                           start=(kc == 0), stop=(kc == KC - 1))
                    if zi:
                        kb.op("act", lambda e, pz=pz, i=i, zi=zi: e.activation(
                            out=stz[i][:, zi * 512:(zi + 1) * 512], in_=pz[:, 0:512], func=AF.Copy),
                            r=[pz_b], w=[stz_b[i]])
                    else:
                        kb.op("dve", lambda e, pz=pz, i=i, zi=zi: e.tensor_copy(
                            out=stz[i][:, zi * 512:(zi + 1) * 512], in_=pz[:, 0:512]), r=[pz_b], w=[stz_b[i]])
                pz, pz_b = bank()
                for kc in range(KC):
                    mm(pz[:, 0:32], nT[:, kc, s * 128:(s + 1) * 128], wab[:, kc, :], [nT_b[kc], wres_b], pz_b,
                       start=(kc == 0), stop=(kc == KC - 1))
                kb.op("dve", lambda e, pz=pz, i=i: e.tensor_copy(out=stz[i][:, 1024:1056], in_=pz[:, 0:32]),
                      r=[pz_b], w=[stz_b[i]])
                kb.dma("pool", ZTM[t0 + s * 128:t0 + (s + 1) * 128, :], stz[i][:, 0:1024], r=[stz_b[i]])
                kb.dma("pool", ABT[t0 + s * 128:t0 + (s + 1) * 128, :], stz[i][:, 1024:1056], r=[stz_b[i]])
        kb.barrier()

    SCALE = float((128 + 64) ** -0.5)
    QT = min(512, T)
    NQT = T // QT
    with ExitStack() as ph:
        km = sb("km", [128, NCH], stack=ph)
        krt = sb("krt", [64, T], BF16, stack=ph)
        res_b = Buf()
        kb.dma("sp", km[:], kmask_in, w=[res_b])
        kb.dma("sp", krt[:], KRT, w=[res_b])
        knt = [sb("knt%d" % i, [128, T], BF16, stack=ph) for i in range(2)]
        qnt = [sb("qnt%d" % i, [128, T], BF16, stack=ph) for i in range(2)]
        qrt = [sb("qrt%d" % i, [64, T], BF16, stack=ph) for i in range(2)]
        vt = [sb("vt%d" % i, [128, NCH, 128], BF16, stack=ph) for i in range(2)]
        hd_b = [Buf(), Buf()]
        pT = [sb("pT%d" % i, [128, QT], BF16, stack=ph) for i in range(3)]
        pT_b = [Buf() for _ in range(3)]
        rden = sb("rden", [128, QT], stack=ph)
        rden_b = Buf()
        dacc = [sb("dacc%d" % i, [128, QT], stack=ph) for i in range(2)]
        dacc_b = [Buf(), Buf()]
        ost = [sb("ost%d" % i, [128, QT], BF16, stack=ph) for i in range(2)]
        ost_b = [Buf(), Buf()]
        n_pt = 0
        n_o = 0
        ps_banks = [(psf[4], psf_b[4]), (psf[5], psf_b[5]),
                    (psb[0][:, :].bitcast(F32), psb_b[0]), (psb[1][:, :].bitcast(F32), psb_b[1])]
        LOOK = 2

        def load_head(h):
            i = h % 2
            kb.dma("sp", knt[i][:], KNT[h], w=[hd_b[i]])
            kb.dma("sp", qnt[i][:], QNT[h], w=[hd_b[i]])
            kb.dma("sp", qrt[i][:], QRT[h], w=[hd_b[i]])
            kb.dma("sp", vt[i][:], VTM[:, h * 128:(h + 1) * 128].rearrange("(c p) d -> p c d", p=128), w=[hd_b[i]])

        def qk_unit(h, qt, kc):
            i = h % 2
            q0 = qt * QT
            ps, ps_b = ps_banks[st_a["pt"] % 4]
            j = st_a["pt"] % 3
            st_a["pt"] += 1
            mm(ps[:, 0:QT], knt[i][:, kc * 128:(kc + 1) * 128], qnt[i][:, q0:q0 + QT], [hd_b[i]], ps_b,
               start=True, stop=False)
            mm(ps[:, 0:QT], krt[:, kc * 128:(kc + 1) * 128], qrt[i][:, q0:q0 + QT], [hd_b[i], res_b], ps_b,
               start=False, stop=True)
            return (h, qt, kc, ps, ps_b, j)

        def pv_unit(u):
            h, qt, kc, ps, ps_b, j = u
            i = h % 2
            q0 = qt * QT
            o = (h * NQT + qt) % 2
            po, po_b = psf[o], psf_b[o]
            pden, pden_b = psf[2 + o], psf_b[2 + o]
            kb.op("act", lambda e: e.activation(out=pT[j][:, :], in_=ps[:, 0:QT], func=AF.Exp, scale=SCALE,
                                                bias=km[:, kc:kc + 1]), r=[ps_b, res_b], w=[pT_b[j]])
            mm(po[:, 0:QT], vt[i][:, kc, :], pT[j][:, :], [hd_b[i], pT_b[j]], po_b,
               start=(kc == 0), stop=(kc == NCH - 1))
            mm(pden[:, 0:QT], onesb[:], pT[j][:, :], [cb, pT_b[j]], pden_b,
               start=(kc == 0), stop=(kc == NCH - 1))
            if kc == NCH - 1:
                kb.op("dve", lambda e: e.reciprocal(out=rden[:, :], in_=pden[:, 0:QT]), r=[pden_b], w=[rden_b])
                jj = o
                kb.op("dve", lambda e: e.tensor_tensor(out=ost[jj][:, :], in0=po[:, 0:QT], in1=rden[:, :],
                                                       op=ALU.mult), r=[po_b, rden_b], w=[ost_b[jj]])
                kb.dma("pool", YT[h, :, q0:q0 + QT], ost[jj][:, :], r=[ost_b[jj]])

        st_a = {"pt": 0}
        pend = []
        load_head(0)
        for h in range(H):
            loaded = False
            for qt in range(NQT):
                for kc in range(NCH):
                    pend.append(qk_unit(h, qt, kc))
                    if len(pend) > LOOK:
                        pv_unit(pend.pop(0))
                    if not loaded and h + 1 < H and all(u[0] == h for u in pend):
                        load_head(h + 1)
                        loaded = True
        while pend:
            pv_unit(pend.pop(0))
        kb.barrier()

    TP = min(2048, T)
    with ExitStack() as ph:
        cw = sb("cw", [128, 24, 5], stack=ph)
        cw_b = Buf()
        for j in range(5):
            kb.dma("sp", cw[:, :, j:j + 1], W["conv_w"][j].rearrange("(c p o) -> p c o", p=128, o=1), w=[cw_b],
                   allow_slow_non_contiguous=True)
        tmk = sb("tmk", [128, T], stack=ph)
        kb.dma("sp", tmk[:], tmask_in.partition_broadcast(128), w=[cw_b])
        xin = [sb("xin%d" % i, [128, TP + 4], stack=ph) for i in range(2)]
        xin_b = [Buf(), Buf()]
        acc = [sb("acc%d" % i, [128, TP], stack=ph) for i in range(2)]
        acc_b = [Buf(), Buf()]
        sq = sb("sq", [128, TP], stack=ph)
        sq_b = Buf()
        rn = sb("rn", [128, TP], stack=ph)
        rn_b = Buf()
        tst = [sb("tst%d" % i, [128, 512], stack=ph) for i in range(2)]
        tst_b = [Buf(), Buf()]
        n_it = 0
        n_ts = 0
        for cc in range(24):
            for pc in range(T // TP):
                c0 = pc * TP
                i = n_it % 2
                n_it += 1
                lo = max(c0 - 2, 0)
                hi = min(c0 + TP + 2, T)
                if lo > c0 - 2:
                    kb.op("pool", lambda e, i=i: e.memset(xin[i][:, 0:2], 0.0), w=[xin_b[i]])
                if hi < c0 + TP + 2:
                    kb.op("pool", lambda e, i=i: e.memset(xin[i][:, TP + 2:TP + 4], 0.0), w=[xin_b[i]])
                kb.dma("sp", xin[i][:, lo - (c0 - 2):hi - (c0 - 2)], GQKV[cc, :, lo:hi], w=[xin_b[i]])
                en = "dve"
                kb.op("act", lambda e, i=i, cc=cc: e.activation(out=acc[i][:, :], in_=xin[i][:, 0:TP], func=AF.Copy,
                                                                scale=cw[:, cc, 0:1]),
                      r=[xin_b[i], cw_b], w=[acc_b[i]])
                for j in range(1, 5):
                    kb.op(en, lambda e, i=i, cc=cc, j=j: e.scalar_tensor_tensor(
                        out=acc[i][:, :], in0=xin[i][:, j:j + TP], scalar=cw[:, cc, j:j + 1], in1=acc[i][:, :],
                        op0=ALU.mult, op1=ALU.add), r=[xin_b[i], cw_b], w=[acc_b[i]])
                kb.op("act", lambda e, i=i: e.activation(out=acc[i][:, :], in_=acc[i][:, :], func=AF.Silu),
                      w=[acc_b[i]])
                kb.op("dve", lambda e, i=i, c0=c0: e.tensor_tensor(out=acc[i][:, :], in0=acc[i][:, :],
                                                                   in1=tmk[:, c0:c0 + TP], op=ALU.mult),
                      r=[cw_b], w=[acc_b[i]])
                if cc < 16:
                    kb.op("act", lambda e, i=i: e.activation(out=sq[:, :], in_=acc[i][:, :], func=AF.Square),
                          r=[acc_b[i]], w=[sq_b])
                    for q in range(TP // min(512, TP)):
                        wd_ = min(512, TP)
                        pss, pss_b = bank()
                        mm(pss[:, 0:wd_], onesf[:], sq[:, q * wd_:(q + 1) * wd_], [cb, sq_b], pss_b)
                        kb.op("act", lambda e, pss=pss, q=q, wd_=wd_: e.activation(
                            out=rn[:, q * wd_:(q + 1) * wd_], in_=pss[:, 0:wd_], func=AF.Sqrt,
                            scale=(128.0 if cc < 8 else 1.0), bias=eps_q[:, 0:1] if cc < 8 else eps_t[:, 0:1]),
                            r=[pss_b, cb], w=[rn_b])
                    kb.op("dve", lambda e: e.reciprocal(out=rn[:, :], in_=rn[:, :]), w=[rn_b])
                    kb.op("dve", lambda e, i=i: e.tensor_tensor(out=acc[i][:, :], in0=acc[i][:, :], in1=rn[:, :],
                                                                op=ALU.mult), r=[rn_b], w=[acc_b[i]])
                    dst = (GQT if cc < 8 else GKT)[cc % 8, :, c0:c0 + TP]
                    kb.dma("pool", dst, acc[i][:, :], r=[acc_b[i]])
                if cc >= 8:
                    hh = cc % 8
                    dstT = GKTM if cc < 16 else GVTM
                    for q in range(TP // 128 // min(4, TP // 128)):
                        nb = min(4, TP // 128)
                        ptb, ptb_b = bank()
                        for b in range(nb):
                            col = (q * nb + b) * 128
                            kb.op("pe", lambda e, ptb=ptb, b=b, col=col, i=i: e.transpose(
                                out=ptb[:, b * 128:(b + 1) * 128], in_=acc[i][:, col:col + 128], identity=idf[:]),
                                r=[acc_b[i], cb], w=[ptb_b], inc=(b == nb - 1))
                        j = n_ts % 2
                        n_ts += 1
                        kb.op("act", lambda e, ptb=ptb, j=j, nb=nb: e.activation(
                            out=tst[j][:, 0:nb * 128], in_=ptb[:, 0:nb * 128], func=AF.Copy), r=[ptb_b], w=[tst_b[j]])
                        r0 = c0 + q * nb * 128
                        kb.dma("pool", dstT[r0:r0 + nb * 128, hh * 128:(hh + 1) * 128].rearrange("(b p) d -> p b d", p=128),
                               tst[j][:, 0:nb * 128].rearrange("p (b d) -> p b d", d=128), r=[tst_b[j]])
        kb.barrier()

    with ExitStack() as ph:
        ab = sb("ab", [128, NCH, 32], stack=ph)
        g_b = Buf()
        kb.dma("sp", ab[:], ABT.rearrange("(c p) n -> p c n", p=128), w=[g_b])
        alog = sb("alog", [128, 16], stack=ph)
        dtb = sb("dtb", [128, 16], stack=ph)
        kb.dma("sp", alog[:], W["a_log"].partition_broadcast(128), w=[g_b])
        kb.dma("sp", dtb[:], W["dt_bias"].partition_broadcast(128), w=[g_b])
        gg = sb("gg", [128, NCH, 16], stack=ph)
        nbeta = sb("nbeta", [128, NCH, 16], stack=ph)
        beta = sb("beta", [128, NCH, 16], stack=ph)
        kb.op("act", lambda e: e.activation(out=alog[:], in_=alog[:], func=AF.Exp), w=[g_b])
        kb.op("dve", lambda e: e.tensor_tensor(out=gg[:], in0=ab[:, :, 0:16],
                                               in1=dtb[:].unsqueeze(1).to_broadcast([128, NCH, 16]), op=ALU.add),
              w=[g_b])
        kb.op("act", lambda e: e.activation(out=gg[:], in_=gg[:], func=AF.Exp), w=[g_b])
        kb.op("act", lambda e: e.activation(out=gg[:], in_=gg[:], func=AF.Ln, bias=1.0), w=[g_b])
        kb.op("dve", lambda e: e.scalar_tensor_tensor(out=gg[:], in0=gg[:], scalar=-1.0,
                                                      in1=alog[:].unsqueeze(1).to_broadcast([128, NCH, 16]),
                                                      op0=ALU.mult, op1=ALU.mult), w=[g_b])
        kb.op("act", lambda e: e.activation(out=beta[:], in_=ab[:, :, 16:32], func=AF.Exp, scale=-1.0), w=[g_b])
        kb.op("dve", lambda e: e.tensor_scalar(out=beta[:], in0=beta[:], scalar1=1.0, scalar2=None, op0=ALU.add),
              w=[g_b])
        kb.op("dve", lambda e: e.reciprocal(out=beta[:], in_=beta[:]), w=[g_b])
        kb.op("dve", lambda e: e.tensor_scalar(out=nbeta[:], in0=beta[:], scalar1=-1.0, scalar2=None, op0=ALU.mult),
              w=[g_b])
        eGi = sb("eGi", [128, 2, NCH, 8], stack=ph)
        neGi = sb("neGi", [128, 2, NCH, 8], stack=ph)
        eGr = sb("eGr", [128, 2, NCH, 8], stack=ph)
        eGt = sb("eGt", [128, 2, NCH, 8], stack=ph)
        CW_ = min(NCH, 64)
        for dr in range(2):
            for c0 in range(0, NCH, CW_):
                for (dst, msk) in ((eGi, (m_le, m_ge)[dr]), (eGr, (m_gt, m_lt)[dr]), (eGt, onesf)):
                    pc_, pc_b = bank()
                    mm(pc_[:, 0:CW_ * 8].rearrange("p (c h) -> p c h", h=8), msk[:], gg[:, c0:c0 + CW_, dr * 8:dr * 8 + 8],
                       [cb, g_b], pc_b)
                    kb.op("act", lambda e, dst=dst, pc_=pc_, dr=dr, c0=c0: e.activation(
                        out=dst[:, dr, c0:c0 + CW_, :], in_=pc_[:, 0:CW_ * 8].rearrange("p (c h) -> p c h", h=8),
                        func=AF.Exp), r=[pc_b], w=[g_b])
        kb.op("dve", lambda e: e.tensor_scalar(out=neGi[:], in0=eGi[:], scalar1=-1.0, scalar2=None, op0=ALU.mult),
              w=[g_b])

        S = sb("S", [128, 16, 128], stack=ph)
        S_b = [Buf() for _ in range(4)]
        kb.op("pool", lambda e: e.memset(S[:], 0.0), w=S_b)
        identb = idf
        qT = [[sb("gq%d%d" % (a, b), [128, 8, 128], stack=ph) for b in range(2)] for a in range(2)]
        kT = [[sb("gk%d%d" % (a, b), [128, 8, 128], stack=ph) for b in range(2)] for a in range(2)]
        ktm = [[sb("gkm%d%d" % (a, b), [128, 8, 128], stack=ph) for b in range(2)] for a in range(2)]
        vtm = [[sb("gvm%d%d" % (a, b), [128, 8, 128], stack=ph) for b in range(2)] for a in range(2)]
        ld_b = [[Buf(), Buf()], [Buf(), Buf()]]

        def wtile(name):
            return sb(name, [128, 4, 128], stack=ph), Buf()

        GR = []
        for dr in range(2):
            for hg in range(2):
                G = {"dr": dr, "hg": hg, "hs": [hg * 4 + x for x in range(4)], "sb": S_b[dr * 2 + hg]}
                for nm in ("P", "Q", "R2", "DS", "LTa", "LTb", "La", "Lb", "R", "qkd"):
                    G[nm], G[nm + "_b"] = wtile("%s%d%d" % (nm, dr, hg))
                GR.append(G)

        def load_step(st):
            for dr in range(2):
                c = st if dr == 0 else NCH - 1 - st
                bi = st % 2
                tb = ld_b[dr][bi]
                sl = slice(c * 128, (c + 1) * 128)
                kb.dma("sp", qT[dr][bi][:], GQT[:, :, sl].rearrange("h p t -> p h t"), w=[tb])
                kb.dma("sp", kT[dr][bi][:], GKT[:, :, sl].rearrange("h p t -> p h t"), w=[tb])
                kb.dma("sp", ktm[dr][bi][:], GKTM[sl, :].rearrange("p (h d) -> p h d", d=128), w=[tb])
                kb.dma("sp", vtm[dr][bi][:], GVTM[sl, :].rearrange("p (h d) -> p h d", d=128), w=[tb])

        b4 = lambda t: t[:].unsqueeze(1).to_broadcast([128, 4, 128])
        r32 = lambda ap: ap.bitcast(F32R)
        m_le_r = sb("m_le_r", [128, 128], stack=ph)
        m_ge_r = sb("m_ge_r", [128, 128], stack=ph)
        kb.op("dve", lambda e: e.tensor_copy(out=r32(m_le_r[:]), in_=m_le[:]), r=[cb], w=[cb])
        kb.op("dve", lambda e: e.tensor_copy(out=r32(m_ge_r[:]), in_=m_ge[:]), r=[cb], w=[cb])
        kb.op("dve", lambda e: e.tensor_copy(out=r32(S[:]), in_=S[:]), w=S_b)

        def mmr(out, lhsT, rhs, rbufs, wbuf):
            kb.op("pe", lambda e: e.matmul(out=out, lhsT=r32(lhsT), rhs=r32(rhs), start=True, stop=True),
                  r=rbufs, w=[wbuf], inc=True)

        def round_loads(st):
            for dr in range(2):
                bi = st % 2
                tb = ld_b[dr][bi]
                kb.op("dve", lambda e: e.tensor_copy(out=r32(kT[dr][bi][:]), in_=kT[dr][bi][:]), w=[tb])
                kb.op("act", lambda e: e.activation(out=r32(qT[dr][bi][:]), in_=qT[dr][bi][:], func=AF.Copy), w=[tb])
                kb.op("dve", lambda e: e.tensor_copy(out=r32(ktm[dr][bi][:]), in_=ktm[dr][bi][:]), w=[tb])
        fl = lambda t: t[:].rearrange("p a b -> p (a b)")

        def ctx(G, st):
            dr = G["dr"]
            G["c"] = st if dr == 0 else NCH - 1 - st
            G["bi"] = st % 2
            G["tb"] = ld_b[dr][st % 2]
            G["m_dm_l"] = (m_gt, m_lt)[dr]
            G["m_dm_r"] = (m_le_r, m_ge_r)[dr]
            G["m_incl"] = (m_le, m_ge)[dr]
            G["m_str"] = (m_lt, m_gt)[dr]

        def s1a(G):
            dr, c = G["dr"], G["c"]
            for x, h in enumerate(G["hs"]):
                kb.op("act", lambda e, x=x, h=h: e.activation(
                    out=r32(G["P"][:, x, :]), in_=G["m_dm_l"][:], func=AF.Copy,
                    scale=gg[:, c, dr * 8 + h:dr * 8 + h + 1]), r=[cb, g_b], w=[G["P_b"]])
            G["pdm"], G["pdm_b"] = bank()
            for x in range(4):
                mmr(G["pdm"][:, x * 128:(x + 1) * 128], G["P"][:, x, :], G["m_dm_r"][:], [G["P_b"], cb], G["pdm_b"])

        def s1b(G):
            kb.op("act", lambda e: e.activation(out=r32(fl(G["Q"])), in_=G["pdm"][:, :], func=AF.Exp),
                  r=[G["pdm_b"]], w=[G["Q_b"]])
            kb.op("dve", lambda e: e.tensor_tensor(out=r32(G["R2"][:]), in0=G["Q"][:], in1=b4(G["m_incl"]), op=ALU.mult),
                  r=[G["Q_b"], cb], w=[G["R2_b"]])
            kb.op("dve", lambda e: e.tensor_tensor(out=G["DS"][:], in0=G["Q"][:], in1=b4(G["m_str"]), op=ALU.mult),
                  r=[G["Q_b"], cb], w=[G["DS_b"]])

        def s2a(G):
            dr, bi, tb = G["dr"], G["bi"], G["tb"]
            G["pkk"], G["pkk_b"] = bank()
            for x, h in enumerate(G["hs"]):
                mmr(G["pkk"][:, x * 128:(x + 1) * 128], kT[dr][bi][:, h, :], kT[dr][bi][:, h, :], [tb], G["pkk_b"])

        def s2a2(G):
            dr, bi, tb = G["dr"], G["bi"], G["tb"]
            G["pqk"], G["pqk_b"] = bank()
            for x, h in enumerate(G["hs"]):
                mmr(G["pqk"][:, x * 128:(x + 1) * 128], kT[dr][bi][:, h, :], qT[dr][bi][:, h, :], [tb], G["pqk_b"])

        def s2b(G):
            dr, c = G["dr"], G["c"]
            kb.op("dve", lambda e: e.tensor_tensor(out=r32(fl(G["LTa"])), in0=G["pkk"][:, :], in1=fl(G["DS"]), op=ALU.mult),
                  r=[G["pkk_b"], G["DS_b"]], w=[G["LTa_b"]])
            for x, h in enumerate(G["hs"]):
                kb.op("act", lambda e, x=x, h=h: e.activation(
                    out=r32(G["LTa"][:, x, :]), in_=G["LTa"][:, x, :], func=AF.Copy,
                    scale=nbeta[:, c, dr * 8 + h:dr * 8 + h + 1]), r=[g_b], w=[G["LTa_b"]])

        def s2b2(G):
            kb.op("dve", lambda e: e.tensor_tensor(out=r32(fl(G["qkd"])), in0=G["pqk"][:, :], in1=fl(G["R2"]), op=ALU.mult),
                  r=[G["pqk_b"], G["R2_b"]], w=[G["qkd_b"]])

        def s3a(G):
            G["ptr"], G["ptr_b"] = bank()
            for x in range(4):
                kb.op("pe", lambda e, x=x: e.transpose(out=G["ptr"][:, x * 128:(x + 1) * 128], in_=G["LTa"][:, x, :],
                                                       identity=idf[:]),
                      r=[G["LTa_b"], cb], w=[G["ptr_b"]], inc=(x == 3))

        def s3b(G):
            kb.op("act", lambda e: e.activation(out=r32(fl(G["La"])), in_=G["ptr"][:, :], func=AF.Copy),
                  r=[G["ptr_b"]], w=[G["La_b"]])
            kb.op("dve", lambda e: e.tensor_tensor(out=r32(G["R"][:]), in0=G["LTa"][:], in1=b4(idf), op=ALU.add),
                  r=[G["LTa_b"], cb], w=[G["R_b"]])
            G["cur"] = "a"

        def lev_a(G, last):
            cur = G["cur"]
            L_, L_b_, LT_, LT_b_ = G["L" + cur], G["L" + cur + "_b"], G["LT" + cur], G["LT" + cur + "_b"]
            G["pL"], G["pL_b"] = bank()
            for x in range(4):
                mmr(G["pL"][:, x * 128:(x + 1) * 128], LT_[:, x, :], L_[:, x, :], [LT_b_, L_b_], G["pL_b"])

        def lev_b(G, last):
            nx = "b" if G["cur"] == "a" else "a"
            kb.op("act", lambda e: e.activation(out=r32(fl(G["L" + nx])), in_=G["pL"][:, :], func=AF.Copy),
                  r=[G["pL_b"]], w=[G["L" + nx + "_b"]])

        def lev_a2(G, last):
            cur = G["cur"]
            L_, L_b_, LT_, LT_b_ = G["L" + cur], G["L" + cur + "_b"], G["LT" + cur], G["LT" + cur + "_b"]
            if not last:
                G["pLT"], G["pLT_b"] = bank()
                for x in range(4):
                    mmr(G["pLT"][:, x * 128:(x + 1) * 128], L_[:, x, :], LT_[:, x, :], [LT_b_, L_b_], G["pLT_b"])

        def lev_b2(G, last):
            nx = "b" if G["cur"] == "a" else "a"
            if not last:
                kb.op("dve", lambda e: e.tensor_copy(out=r32(fl(G["LT" + nx])), in_=G["pLT"][:, :]),
                      r=[G["pLT_b"]], w=[G["LT" + nx + "_b"]])
            G["cur"] = nx

        def lev_c(G, last):
            cur = G["cur"]
            G["pR"], G["pR_b"] = bank()
            for x in range(4):
                mmr(G["pR"][:, x * 128:(x + 1) * 128], G["L" + cur][:, x, :], G["R"][:, x, :],
                   [G["L" + cur + "_b"], G["R_b"]], G["pR_b"])

        def lev_d(G, last):
            kb.op("dve", lambda e: e.tensor_tensor(out=r32(fl(G["R"])), in0=G["pR"][:, :], in1=fl(G["R"]), op=ALU.add),
                  r=[G["pR_b"]], w=[G["R_b"]])

        def s5a(G):
            dr, bi, tb = G["dr"], G["bi"], G["tb"]
            G["pks"], G["pks_b"] = bank()
            for x, h in enumerate(G["hs"]):
                mmr(G["pks"][:, x * 128:(x + 1) * 128], kT[dr][bi][:, h, :], S[:, dr * 8 + h, :], [tb, G["sb"]], G["pks_b"])

        def s5a2(G):
            dr, bi, tb = G["dr"], G["bi"], G["tb"]
            G["pqs"], G["pqs_b"] = bank()
            for x, h in enumerate(G["hs"]):
                mmr(G["pqs"][:, x * 128:(x + 1) * 128], qT[dr][bi][:, h, :], S[:, dr * 8 + h, :], [tb, G["sb"]], G["pqs_b"])

        def s5b(G):
            dr, bi, tb, c = G["dr"], G["bi"], G["tb"], G["c"]
            for x, h in enumerate(G["hs"]):
                kb.op("dve", lambda e, x=x, h=h: e.scalar_tensor_tensor(
                    out=r32(G["P"][:, x, :]), in0=G["pks"][:, x * 128:(x + 1) * 128], scalar=neGi[:, dr, c, h:h + 1],
                    in1=vtm[dr][bi][:, h, :], op0=ALU.mult, op1=ALU.add), r=[G["pks_b"], g_b, tb], w=[G["P_b"]])

        def s5b2(G):
            kb.op("act", lambda e: e.activation(out=fl(G["DS"]), in_=G["pqs"][:, :], func=AF.Copy),
                  r=[G["pqs_b"]], w=[G["DS_b"]])

        def s5c(G):
            G["pvn"], G["pvn_b"] = bank()
            for x in range(4):
                mmr(G["pvn"][:, x * 128:(x + 1) * 128], G["R"][:, x, :], G["P"][:, x, :], [G["R_b"], G["P_b"]], G["pvn_b"])

        def s5d(G):
            dr, c = G["dr"], G["c"]
            for x, h in enumerate(G["hs"]):
                kb.op("act", lambda e, x=x, h=h: e.activation(
                    out=r32(G["Q"][:, x, :]), in_=G["pvn"][:, x * 128:(x + 1) * 128], func=AF.Copy,
                    scale=beta[:, c, dr * 8 + h:dr * 8 + h + 1]), r=[G["pvn_b"], g_b], w=[G["Q_b"]])
            for x, h in enumerate(G["hs"]):
                kb.op("act", lambda e, x=x, h=h: e.activation(
                    out=r32(G["R2"][:, x, :]), in_=G["Q"][:, x, :], func=AF.Copy, scale=eGr[:, dr, c, h:h + 1]),
                    r=[G["Q_b"], g_b], w=[G["R2_b"]])

        def s5e(G):
            dr, bi, tb = G["dr"], G["bi"], G["tb"]
            G["po"], G["po_b"] = bank()
            for x in range(4):
                mmr(G["po"][:, x * 128:(x + 1) * 128], G["qkd"][:, x, :], G["Q"][:, x, :], [G["qkd_b"], G["Q_b"]], G["po_b"])

        def s5e2(G):
            dr, bi, tb = G["dr"], G["bi"], G["tb"]
            G["pds"], G["pds_b"] = bank()
            for x, h in enumerate(G["hs"]):
                mmr(G["pds"][:, x * 128:(x + 1) * 128], ktm[dr][bi][:, h, :], G["R2"][:, x, :], [tb, G["R2_b"]], G["pds_b"])

        def s5f2(G):
            dr, c, hg = G["dr"], G["c"], G["hg"]
            for x, h in enumerate(G["hs"]):
                kb.op("dve", lambda e, x=x, h=h: e.scalar_tensor_tensor(
                    out=r32(S[:, dr * 8 + h, :]), in0=S[:, dr * 8 + h, :], scalar=eGt[:, dr, c, h:h + 1],
                    in1=G["pds"][:, x * 128:(x + 1) * 128], op0=ALU.mult, op1=ALU.add), r=[G["pds_b"], g_b], w=[G["sb"]])

        def s5f(G):
            dr, c, hg = G["dr"], G["c"], G["hg"]
            for x, h in enumerate(G["hs"]):
                kb.op("dve", lambda e, x=x, h=h: e.scalar_tensor_tensor(
                    out=G["DS"][:, x, :], in0=G["DS"][:, x, :], scalar=eGi[:, dr, c, h:h + 1],
                    in1=G["po"][:, x * 128:(x + 1) * 128], op0=ALU.mult, op1=ALU.add), r=[G["po_b"], g_b], w=[G["DS_b"]])
            kb.dma("sp", (OF, OB)[dr][c * 128:(c + 1) * 128, hg * 512:(hg + 1) * 512], fl(G["DS"]), r=[G["DS_b"]])

        stages = [s1a, s1b, s2a, s2b, s2a2, s2b2, s3a, s3b]
        for lev in range(6):
            last = (lev == 5)
            for f_ in (lev_a, lev_b, lev_a2, lev_b2, lev_c, lev_d):
                stages.append(lambda G, f_=f_, last=last: f_(G, last))
        stages += [s5a, s5b, s5a2, s5b2, s5c, s5d, s5e, s5f, s5e2, s5f2]

        load_step(0)
        for st in range(NCH):
            if st + 1 < NCH:
                load_step(st + 1)
            round_loads(st)
            for G in GR:
                ctx(G, st)
            for stg in stages:
                for G in GR:
                    stg(G)
        kb.barrier()

    with ExitStack() as ph:
        gdn_g = sb("gdn_g", [128, 128], stack=ph)
        c_b = Buf()
        kb.dma("sp", gdn_g[:], W["gdn_norm"].partition_broadcast(128), w=[c_b])
        of_ = [sb("of%d" % i, [128, 1024], stack=ph) for i in range(2)]
        ob_ = [sb("ob%d" % i, [128, 1024], stack=ph) for i in range(2)]
        zz = [sb("zz%d" % i, [128, 1024], stack=ph) for i in range(2)]
        o_b = [Buf(), Buf()]
        ob16 = [sb("ob16%d" % i, [128, 1024], BF16, stack=ph) for i in range(2)]
        ob16_b = [Buf(), Buf()]
        yst = [sb("yst%d" % i, [128, 8, 128], BF16, stack=ph) for i in range(2)]
        yst_b = [Buf(), Buf()]
        ss3 = sb("ss3", [128, 16], stack=ph)
        ss3_b = [Buf(), Buf()]
        hd3 = lambda ap: ap.rearrange("p (h d) -> p h d", d=128)
        for c in range(NCH):
            i = c % 2
            r0 = c * 128
            kb.dma("sp", of_[i][:], OF[r0:r0 + 128, :], w=[o_b[i]])
            kb.dma("sp", ob_[i][:], OB[r0:r0 + 128, :], w=[o_b[i]])
            kb.dma("sp", zz[i][:], ZTM[r0:r0 + 128, :], w=[o_b[i]])
            sv = ss3[:, i * 8:(i + 1) * 8]
            kb.op("dve", lambda e: e.tensor_tensor(out=of_[i][:], in0=of_[i][:], in1=ob_[i][:], op=ALU.add), w=[o_b[i]])
            kb.op("act", lambda e: e.activation(out=ob_[i][:], in_=of_[i][:], func=AF.Square), w=[o_b[i]])
            kb.op("dve", lambda e: e.tensor_reduce(out=sv, in_=hd3(ob_[i][:]), axis=mybir.AxisListType.X, op=ALU.add),
                  r=[o_b[i]], w=[ss3_b[i]])
            kb.op("act", lambda e: e.activation(out=sv, in_=sv, func=AF.Sqrt, scale=1.0 / 128, bias=eps_t[:, 0:1]),
                  r=[cb], w=[ss3_b[i]])
            kb.op("dve", lambda e: e.reciprocal(out=sv, in_=sv), w=[ss3_b[i]])
            kb.op("act", lambda e: e.activation(out=zz[i][:], in_=zz[i][:], func=AF.Silu), w=[o_b[i]])
            kb.op("dve", lambda e: e.tensor_tensor(out=hd3(of_[i][:]), in0=hd3(of_[i][:]),
                                                   in1=sv.unsqueeze(2).to_broadcast([128, 8, 128]), op=ALU.mult),
                  r=[ss3_b[i]], w=[o_b[i]])
            kb.op("dve", lambda e: e.tensor_tensor(out=hd3(zz[i][:]), in0=hd3(zz[i][:]),
                                                   in1=gdn_g[:].unsqueeze(1).to_broadcast([128, 8, 128]), op=ALU.mult),
                  r=[c_b], w=[o_b[i]])
            kb.op("dve", lambda e: e.tensor_tensor(out=ob16[i][:], in0=of_[i][:], in1=zz[i][:], op=ALU.mult),
                  r=[o_b[i]], w=[ob16_b[i]])
            for half in range(2):
                ptb, ptb_b = tbank()
                for q in range(4):
                    hh = half * 4 + q
                    kb.op("pe", lambda e, q=q, hh=hh: e.transpose(
                        out=ptb[:, q * 128:(q + 1) * 128], in_=ob16[i][:, hh * 128:(hh + 1) * 128], identity=idb[:]),
                        r=[ob16_b[i], cb], w=[ptb_b], inc=(q == 3))
                kb.op("act", lambda e, half=half: e.activation(
                    out=yst[i][:, half * 4:(half + 1) * 4, :], in_=ptb[:, 0:512].rearrange("p (a b) -> p a b", b=128),
                    func=AF.Copy), r=[ptb_b], w=[yst_b[i]])
            kb.dma("pool", YT[8:16, :, r0:r0 + 128].rearrange("h p t -> p h t"), yst[i][:], r=[yst_b[i]])
        kb.barrier()

    with ExitStack() as ph:
        d = alloc_tok(ph)
        xt, xt_b, nT, nT_b = d["xt"], d["xt_b"], d["nT"], d["nT_b"]
        gfin = sb("gfin", [128, D], stack=ph)
        c_b = Buf()
        kb.dma("sp", gfin[:], W["norm_final"].partition_broadcast(128), w=[c_b])
        hm, hm_b = d["hm"], d["hm_b"]
        yT = carve(hm, 0, [128, KC, TT], BF16)
        yT_b = [Buf() for _ in range(KC)]
        of_ = carve(hm, 16384, [128, 1024], F32)
        ob_ = carve(hm, 20480, [128, 1024], F32)
        zz = carve(hm, 24576, [128, 1024], F32)
        o_b = Buf()
        ob16 = carve(hm, 28672, [128, NS, 1024], BF16)
        ob16_b = [Buf() for _ in range(NS)]
        wple = carve(hm, 0, [128, 2, 2048], BF16)
        pt_ = carve(hm, 8192, [128, NS, PLE], F32)
        pb16 = carve(hm, 12288, [128, NS, PLE], BF16)
        pT = carve(hm, 14336, [128, 2, TT], BF16)
        gate = [carve(hm, 16384 + i * 2048, [128, 512], F32) for i in range(2)]
        pt_b = Buf()
        pb16_b = Buf()
        pT_b = Buf()
        gate_b = [Buf(), Buf()]
        ss3 = sb("ss3", [128, 8], stack=ph)
        ss3_b = Buf()
        pre_bufs = yT_b + [o_b] + ob16_b
        post_bufs = [pt_b, pb16_b, pT_b] + gate_b
        wo = d["wd"]
        wo_b = d["wd_b"]
        n_g = 0
        for t in range(NT):
            t0 = t * TT
            for s in range(NS):
                kb.dma("sp", xt[:, s, :], H1[t0 + s * 128:t0 + (s + 1) * 128, :], w=[xt_b[s]])
            for bb in pre_bufs:
                bb.w = {}
                bb.r = {}
            merge(pre_bufs, hm_b + post_bufs)
            kb.dma("sp", yT[:, 0:8, :], YT[0:8, :, t0:t0 + TT].rearrange("h p t -> p h t"), w=yT_b[0:8])
            kb.dma("sp", yT[:, 8:16, :], YT[8:16, :, t0:t0 + TT].rearrange("h p t -> p h t"), w=yT_b[8:16])
            for g in range(4):
                i = g % 2
                wov = carve(wo[i], 0, [128, KC, 512], BF16)
                kb.dma("sp", wov, WOUT[g], w=[wo_b[i]])
                for s in range(NS):
                    pd, pd_b = bank()
                    for kc in range(KC):
                        mm(pd[:, 0:512], yT[:, kc, s * 128:(s + 1) * 128], wov[:, kc, :], [yT_b[kc], wo_b[i]], pd_b,
                           start=(kc == 0), stop=(kc == KC - 1))
                    kb.op("dve", lambda e, pd=pd, s=s, g=g: e.tensor_tensor(
                        out=xt[:, s, g * 512:(g + 1) * 512], in0=pd[:, 0:512], in1=xt[:, s, g * 512:(g + 1) * 512],
                        op=ALU.add), r=[pd_b], w=[xt_b[s]])
            do_norm(d)
            merge(hm_b, pre_bufs)
            do_ffn(d, "g2", "u2", "d2")
            for bb in post_bufs:
                bb.w = {}
                bb.r = {}
            merge(post_bufs, hm_b)
            kb.dma("sp", wple, WPLE, w=[pt_b])
            kb.dma("sp", pt_, p_in[t0:t0 + TT, :].rearrange("(s p) d -> p s d", p=128), w=[pt_b])
            do_norm(d)
            kb.op("dve", lambda e: e.tensor_copy(out=pb16, in_=pt_), r=[pt_b], w=[pb16_b])
            for kc in range(2):
                ptb, ptb_b = tbank()
                for s in range(NS):
                    kb.op("pe", lambda e, ptb=ptb, s=s, kc=kc: e.transpose(
                        out=ptb[:, s * 128:(s + 1) * 128], in_=pb16[:, s, kc * 128:(kc + 1) * 128], identity=idb[:]),
                        r=[pb16_b, cb], w=[ptb_b], inc=(s == NS - 1))
                kb.op("act", lambda e, ptb=ptb, kc=kc: e.activation(out=pT[:, kc, 0:TT], in_=ptb[:, 0:TT], func=AF.Copy),
                      r=[ptb_b], w=[pT_b])
            for g in range(4):
                i = g % 2
                wov = carve(wo[i], 0, [128, KC, 512], BF16)
                kb.dma("sp", wov, WPG[g], w=[wo_b[i]])
                for s in range(NS):
                    pg, pg_b = bank()
                    for kc in range(KC):
                        mm(pg[:, 0:512], nT[:, kc, s * 128:(s + 1) * 128], wov[:, kc, :], [nT_b[kc], wo_b[i]], pg_b,
                           start=(kc == 0), stop=(kc == KC - 1))
                    pp, pp_b = bank()
                    for kc in range(2):
                        mm(pp[:, 0:512], pT[:, kc, s * 128:(s + 1) * 128], wple[:, kc, g * 512:(g + 1) * 512],
                           [pT_b, pt_b], pp_b, start=(kc == 0), stop=(kc == 1))
                    j = n_g % 2
                    n_g += 1
                    kb.op("act", lambda e, pg=pg, j=j: e.activation(out=gate[j], in_=pg[:, 0:512], func=AF.Sigmoid),
                          r=[pg_b], w=[gate_b[j]])
                    kb.op("dve", lambda e, pp=pp, j=j: e.tensor_tensor(out=gate[j], in0=pp[:, 0:512], in1=gate[j],
                                                                       op=ALU.mult), r=[pp_b], w=[gate_b[j]])
                    kb.op("dve", lambda e, j=j, s=s, g=g: e.tensor_tensor(
                        out=xt[:, s, g * 512:(g + 1) * 512], in0=xt[:, s, g * 512:(g + 1) * 512], in1=gate[j],
                        op=ALU.add), r=[gate_b[j]], w=[xt_b[s]])
            for s in range(NS):
                kb.op("act", lambda e, s=s: e.activation(out=d["junk"][:, 0:D], in_=xt[:, s, :], func=AF.Square,
                                                         accum_out=d["ssq"][:, s:s + 1]),
                      r=[xt_b[s]], w=[d["junk_b"], d["ssq_b"][s]])
                kb.op("act", lambda e, s=s: e.activation(out=d["ssq"][:, s:s + 1], in_=d["ssq"][:, s:s + 1],
                                                         func=AF.Sqrt, scale=1.0 / D, bias=eps_t[:, 0:1]),
                      r=[cb], w=[d["ssq_b"][s]])
                kb.op("dve", lambda e, s=s: e.reciprocal(out=d["ssq"][:, s:s + 1], in_=d["ssq"][:, s:s + 1]),
                      w=[d["ssq_b"][s]])
                kb.op("dve", lambda e, s=s: e.scalar_tensor_tensor(
                    out=xt[:, s, :], in0=xt[:, s, :], scalar=d["ssq"][:, s:s + 1], in1=gfin[:], op0=ALU.mult,
                    op1=ALU.mult), r=[d["ssq_b"][s], c_b], w=[xt_b[s]])
            for s in range(NS):
                kb.dma("pool", y_out[t0 + s * 128:t0 + (s + 1) * 128, :], xt[:, s, :], r=[xt_b[s]])
        kb.barrier()
    return nc, es


def _rope_tables(T):
    inv = 1.0 / (10000.0 ** (np.arange(0, 64, 2, dtype=np.float32) / 64.0))
    ang = np.arange(T, dtype=np.float32)[:, None] * inv[None, :].astype(np.float32)
    c = np.cos(ang).astype(np.float32).T
    s = np.sin(ang).astype(np.float32).T
    return (np.ascontiguousarray(np.concatenate([c, c], 0)),
            np.ascontiguousarray(np.concatenate([-s, s], 0)))


_CACHE = {}


def core_inputs(xs, ps, Tpad, weights):
    S = xs.shape[0]
    x = np.zeros((Tpad, D), np.float32)
    x[:S] = xs
    p = np.zeros((Tpad, PLE), np.float32)
    p[:S] = ps
    km = np.zeros((Tpad,), np.float32)
    km[S:] = -60.0
    km = np.ascontiguousarray(km.reshape(Tpad // 128, 128).T)
    c2, s2 = _rope_tables(Tpad)
    tm = np.zeros((Tpad,), np.float32)
    tm[:S] = 1.0
    m = {"x": x, "p": p, "kmask": km, "tmask": tm, "cos2": c2, "sin2": s2}
    m.update(weights)
    return m


def kernel(**inputs):
    inp = {k: np.asarray(v) for k, v in inputs.items()}
    B, S, _ = inp["x_prompt"].shape
    DB, DS, _ = inp["x_sample"].shape
    T = S
    weights = {}
    for k, v in inp.items():
        if k in ("x_prompt", "x_sample", "p_prompt", "p_sample"):
            continue
        a = np.ascontiguousarray(v, dtype=np.float32)
        if k != "norm_final":
            a = a[0]
        if k in ("a_log", "dt_bias"):
            a = np.ascontiguousarray(a.reshape(16))
        weights[k] = a
    in_maps = []
    for i in range(B):
        in_maps.append(core_inputs(inp["x_prompt"][i], inp["p_prompt"][0, i], T, weights))
    for i in range(DB):
        in_maps.append(core_inputs(inp["x_sample"][i], inp["p_sample"][0, i], T, weights))
    key = (T,)
    if key not in _CACHE:
        _CACHE[key] = build(T, min(512, T))
    nc, _es = _CACHE[key]
    res = run_bass_kernel_spmd(nc, in_maps, core_ids=list(range(len(in_maps))))
    yp = np.stack([np.asarray(res.results[i]["y"])[:S] for i in range(B)]).astype(np.float32)
    ys = np.stack([np.asarray(res.results[B + i]["y"])[:DS] for i in range(DB)]).astype(np.float32)
    return (yp, ys)
```
